# Optimizing a Trainium2 kernel written in Bass

```python
import math
import jax, jax.numpy as jnp
from jax import lax
import numpy as np

D_MODEL = 1024
BATCH = 2
SEQ = 8192
DEPTH = 1

CHUNK = 64
D_SSM = D_MODEL // 2
SSM_GROUP = 16
N_SSM_GROUPS = D_SSM // SSM_GROUP
SSM_STATE = 64
D_RWKV = D_MODEL // 2
RWKV_HEAD = 64
N_RWKV_HEADS = D_RWKV // RWKV_HEAD
DECAY_LORA = 64
AAA_LORA = 64
GATE_LORA = 128
D_RWKV_IN = 3 * D_RWKV + GATE_LORA + DECAY_LORA + AAA_LORA
D_IN = D_SSM + D_RWKV_IN + 2 * D_MODEL
D_FF = 4 * D_MODEL
RMS_EPS = 1e-6
GN_EPS = 64e-5
L2_EPS = 1e-12
MIN_NEG_REAL = -1e-4

kernel_name = "hybrid_s5_rwkv7_gated_block"


def rmsnorm(x, g):
    xf = x.astype(jnp.float32)
    y = xf * lax.rsqrt(jnp.mean(xf * xf, axis=-1, keepdims=True) + RMS_EPS)
    return (y * g.astype(jnp.float32)).astype(x.dtype)


def s5_branch(u, lam_re, lam_im, log_dt, b_re, b_im, c_re, c_im, d_skip, w_glu, b_glu):
    bsz, seq, _ = u.shape
    uf = u.astype(jnp.float32).reshape(bsz, seq, N_SSM_GROUPS, SSM_GROUP)
    lr = jnp.minimum(lam_re.astype(jnp.float32), MIN_NEG_REAL)
    li = lam_im.astype(jnp.float32)
    dt = jnp.exp(log_dt.astype(jnp.float32))[:, None]
    mag = jnp.exp(lr * dt)
    ab_re = mag * jnp.cos(li * dt)
    ab_im = mag * jnp.sin(li * dt)
    den = lr * lr + li * li
    xm1 = ab_re - 1.0
    q_re = (xm1 * lr + ab_im * li) / den
    q_im = (ab_im * lr - xm1 * li) / den
    br = b_re.astype(jnp.float32)
    bi = b_im.astype(jnp.float32)
    bb_re = q_re[..., None] * br - q_im[..., None] * bi
    bb_im = q_re[..., None] * bi + q_im[..., None] * br
    bu_re = jnp.einsum('gpm,bsgm->bsgp', bb_re, uf)
    bu_im = jnp.einsum('gpm,bsgm->bsgp', bb_im, uf)
    a_re = jnp.broadcast_to(ab_re, bu_re.shape)
    a_im = jnp.broadcast_to(ab_im, bu_im.shape)

    def combine(e1, e2):
        a1r, a1i, b1r, b1i = e1
        a2r, a2i, b2r, b2i = e2
        return (a2r * a1r - a2i * a1i,
                a2r * a1i + a2i * a1r,
                a2r * b1r - a2i * b1i + b2r,
                a2r * b1i + a2i * b1r + b2i)

    _, _, xs_re, xs_im = lax.associative_scan(combine, (a_re, a_im, bu_re, bu_im), axis=1)
    y = (jnp.einsum('gmp,bsgp->bsgm', c_re.astype(jnp.float32), xs_re)
         - jnp.einsum('gmp,bsgp->bsgm', c_im.astype(jnp.float32), xs_im)
         + d_skip.astype(jnp.float32).reshape(N_SSM_GROUPS, SSM_GROUP) * uf)
    y = y.reshape(bsz, seq, D_SSM)
    z = jax.nn.gelu(y)
    out = z * jax.nn.sigmoid(z @ w_glu.astype(jnp.float32) + b_glu.astype(jnp.float32))
    return out.astype(u.dtype)


def rwkv7_branch(p, mu, w0, w2, a0, a2, g2, k_k, k_a, r_k, lnx_g, lnx_b):
    bsz, seq, _ = p.shape
    pf = p.astype(jnp.float32)
    prev = jnp.pad(pf[:, :-1], ((0, 0), (1, 0), (0, 0)))
    pm = pf + (prev - pf) * mu.astype(jnp.float32)
    o1 = D_RWKV
    o2 = 2 * D_RWKV
    o3 = 3 * D_RWKV
    o4 = o3 + GATE_LORA
    o5 = o4 + DECAY_LORA
    r = pm[..., :o1]
    k = pm[..., o1:o2]
    v = pm[..., o2:o3]
    xg = pm[..., o3:o4]
    xw = pm[..., o4:o5]
    xa = pm[..., o5:]
    w = -jax.nn.softplus(-(w0 + jnp.tanh(xw) @ w2.astype(jnp.float32))) - 0.5
    decay = jnp.exp(-jnp.exp(w))
    a = jax.nn.sigmoid(a0 + xa @ a2.astype(jnp.float32))
    g = jax.nn.sigmoid(xg) @ g2.astype(jnp.float32)

    hs = (bsz, seq, N_RWKV_HEADS, RWKV_HEAD)
    r = r.reshape(hs)
    k = k.reshape(hs)
    v = v.reshape(hs)
    decay = decay.reshape(hs)
    a = a.reshape(hs)
    kk = k * k_k.astype(jnp.float32).reshape(N_RWKV_HEADS, RWKV_HEAD)
    kk = kk / jnp.maximum(jnp.linalg.norm(kk, axis=-1, keepdims=True), L2_EPS)
    k = k * (1.0 + (a - 1.0) * k_a.astype(jnp.float32).reshape(N_RWKV_HEADS, RWKV_HEAD))
    a_vec = -kk
    b_vec = kk * a

    n_chunks = seq // CHUNK

    def to_chunks(t):
        return jnp.transpose(t, (1, 0, 2, 3)).reshape(n_chunks, CHUNK, bsz, N_RWKV_HEADS, RWKV_HEAD)

    def step(state, inp):
        rt, wt, kt, vt, at, bt = inp
        sa = jnp.einsum('bhvk,bhk->bhv', state, at)
        state = (state * wt[:, :, None, :] + sa[..., None] * bt[:, :, None, :]
                 + vt[..., None] * kt[:, :, None, :])
        yt = jnp.einsum('bhvk,bhk->bhv', state, rt)
        return state, yt

    def chunk_step(state, chunk_inp):
        return lax.scan(step, state, chunk_inp)

    state0 = jnp.zeros((bsz, N_RWKV_HEADS, RWKV_HEAD, RWKV_HEAD), jnp.float32)
    xs = tuple(to_chunks(t) for t in (r, decay, k, v, a_vec, b_vec))
    _, ys = lax.scan(chunk_step, state0, xs)
    y = jnp.transpose(ys.reshape(seq, bsz, N_RWKV_HEADS, RWKV_HEAD), (1, 0, 2, 3))

    mean = jnp.mean(y, axis=-1, keepdims=True)
    var = jnp.mean(jnp.square(y - mean), axis=-1, keepdims=True)
    y = (y - mean) * lax.rsqrt(var + GN_EPS)
    y = (y * lnx_g.astype(jnp.float32).reshape(N_RWKV_HEADS, RWKV_HEAD)
         + lnx_b.astype(jnp.float32).reshape(N_RWKV_HEADS, RWKV_HEAD))
    bonus = jnp.sum(r * k * r_k.astype(jnp.float32), axis=-1, keepdims=True) * v
    y = (y + bonus).reshape(bsz, seq, D_RWKV) * g
    return y.astype(p.dtype)


def setup_inputs(seed: int = 0) -> dict:
    key = jax.random.key(seed)
    ks = jax.random.split(key, 32)
    L = DEPTH
    G, P, M = N_SSM_GROUPS, SSM_STATE, SSM_GROUP
    f32 = jnp.float32
    nrm = lambda k, s: jax.random.normal(k, s, f32)
    x = nrm(ks[0], (BATCH, SEQ, D_MODEL))
    norm1_g = 1.0 + 0.02 * nrm(ks[1], (L, D_MODEL))
    w_in = nrm(ks[2], (L, D_MODEL, D_IN)) * D_MODEL ** -0.5
    lam_re = -0.5 + 0.01 * nrm(ks[3], (L, G, P))
    lam_im = math.pi * jnp.arange(P, dtype=f32)[None, None, :] + 0.01 * nrm(ks[4], (L, G, P))
    log_dt = jax.random.uniform(ks[5], (L, G), f32, math.log(1e-3), math.log(1e-1))
    b_re = nrm(ks[6], (L, G, P, M)) * (2.0 * M) ** -0.5
    b_im = nrm(ks[7], (L, G, P, M)) * (2.0 * M) ** -0.5
    c_re = 0.5 * nrm(ks[8], (L, G, M, P))
    c_im = 0.5 * nrm(ks[9], (L, G, M, P))
    d_skip = nrm(ks[10], (L, D_SSM))
    w_glu = nrm(ks[11], (L, D_SSM, D_SSM)) * D_SSM ** -0.5
    b_glu = 0.01 * nrm(ks[12], (L, D_SSM))
    mu_rwkv = jax.random.uniform(ks[13], (L, D_RWKV_IN), f32)
    w0 = jax.random.uniform(ks[14], (L, D_RWKV), f32, -6.5, -1.5)
    w2 = 0.1 * nrm(ks[15], (L, DECAY_LORA, D_RWKV))
    a0 = 0.1 * nrm(ks[16], (L, D_RWKV))
    a2 = 0.1 * nrm(ks[17], (L, AAA_LORA, D_RWKV))
    g2 = nrm(ks[18], (L, GATE_LORA, D_RWKV)) * GATE_LORA ** -0.5
    k_k = 0.85 + 0.02 * nrm(ks[19], (L, D_RWKV))
    k_a = 1.0 + 0.02 * nrm(ks[20], (L, D_RWKV))
    r_k = 0.1 * nrm(ks[21], (L, N_RWKV_HEADS, RWKV_HEAD))
    lnx_g = 1.0 + 0.02 * nrm(ks[22], (L, D_RWKV))
    lnx_b = 0.01 * nrm(ks[23], (L, D_RWKV))
    w_branch = nrm(ks[24], (L, D_SSM + D_RWKV, D_MODEL)) * (D_SSM ** -0.5)
    w_out = nrm(ks[25], (L, D_MODEL, D_MODEL)) * D_MODEL ** -0.5
    norm2_g = 1.0 + 0.02 * nrm(ks[26], (L, D_MODEL))
    w_ff1 = nrm(ks[27], (L, D_MODEL, D_FF)) * D_MODEL ** -0.5
    w_ff2 = nrm(ks[28], (L, D_FF, D_MODEL)) * D_FF ** -0.5
    norm_f_g = 1.0 + 0.02 * nrm(ks[29], (D_MODEL,))
    return {"x": x, "norm1_g": norm1_g, "w_in": w_in,
            "lam_re": lam_re, "lam_im": lam_im, "log_dt": log_dt,
            "b_re": b_re, "b_im": b_im, "c_re": c_re, "c_im": c_im,
            "d_skip": d_skip, "w_glu": w_glu, "b_glu": b_glu,
            "mu_rwkv": mu_rwkv, "w0": w0, "w2": w2, "a0": a0, "a2": a2, "g2": g2,
            "k_k": k_k, "k_a": k_a, "r_k": r_k, "lnx_g": lnx_g, "lnx_b": lnx_b,
            "w_branch": w_branch, "w_out": w_out, "norm2_g": norm2_g,
            "w_ff1": w_ff1, "w_ff2": w_ff2, "norm_f_g": norm_f_g}


def reference(x, norm1_g, w_in, lam_re, lam_im, log_dt, b_re, b_im, c_re, c_im,
              d_skip, w_glu, b_glu, mu_rwkv, w0, w2, a0, a2, g2, k_k, k_a, r_k,
              lnx_g, lnx_b, w_branch, w_out, norm2_g, w_ff1, w_ff2, norm_f_g):
    s1 = D_SSM
    s2 = D_SSM + D_RWKV_IN
    s3 = s2 + D_MODEL
    for l in range(DEPTH):
        h = rmsnorm(x, norm1_g[l])
        proj = h @ w_in[l]
        u = proj[..., :s1]
        p_rwkv = proj[..., s1:s2]
        gate_a = jax.nn.sigmoid(proj[..., s2:s3])
        gate_b = jax.nn.sigmoid(proj[..., s3:])
        o_a = s5_branch(u, lam_re[l], lam_im[l], log_dt[l], b_re[l], b_im[l],
                        c_re[l], c_im[l], d_skip[l], w_glu[l], b_glu[l])
        o_b = rwkv7_branch(p_rwkv, mu_rwkv[l], w0[l], w2[l], a0[l], a2[l], g2[l],
                           k_k[l], k_a[l], r_k[l], lnx_g[l], lnx_b[l])
        wb = w_branch[l]
        merged = gate_a * (o_a @ wb[:D_SSM]) + gate_b * (o_b @ wb[D_SSM:])
        x = x + merged @ w_out[l]
        h2 = rmsnorm(x, norm2_g[l])
        x = x + jnp.square(jax.nn.relu(h2 @ w_ff1[l])) @ w_ff2[l]
    return rmsnorm(x, norm_f_g)
```

```python
import math
from contextlib import ExitStack

import numpy as np
import concourse.bass as bass
import concourse.mybir as mybir
from concourse.bass_utils import run_bass_kernel_spmd

F32 = mybir.dt.float32
BF16 = mybir.dt.bfloat16
I32 = mybir.dt.int32
AF = mybir.ActivationFunctionType
ALU = mybir.AluOpType
AX = mybir.AxisListType

D = 1024
KC = 8
TOK = 128
SEQ = 8192
NSEG = 4
NT_FULL = SEQ // NSEG // TOK
D_SSM = 512
D_RWKV = 512
NH = 8
HD = 64
D_RIN = 1792
D_IN = 4352
D_FF = 4096
RMS_EPS = 1e-6
GN_EPS = 64e-5
NEG_EH = -math.exp(-0.5)
TWO_PI = 2.0 * math.pi
SAME_ENGINE_WAITS = True
import os
SEQ_MODE = int(os.environ.get('SEQ_MODE', '0'))
TOG = os.environ.get('TOG', '')
RW_STOP = int(os.environ.get('RW_STOP', '99'))
FSP_W = int(os.environ.get('FSP_W', '3'))
FSP_FIRST = int(os.environ.get('FSP_FIRST', '1'))
INLINE_WAITS = int(os.environ.get('INLINE_WAITS', '1'))
B1SEQ = int(os.environ.get('B1SEQ', '0'))
B2SEQ = int(os.environ.get('B2SEQ', '0'))


class Buf:
    __slots__ = ("name", "w", "r", "excl")

    def __init__(self, name, excl=False):
        self.name = name
        self.w = None
        self.r = {}
        self.excl = excl


class V:
    __slots__ = ("ap", "buf")

    def __init__(self, ap, buf):
        self.ap = ap
        self.buf = buf

    def __getitem__(self, idx):
        return V(self.ap[idx], self.buf)

    def bitcast(self, dt):
        return V(self.ap.bitcast(dt), self.buf)


class KB:
    NDS = 8

    def __init__(self, nc):
        self.nc = nc
        self.eng = {"pe": nc.tensor, "act": nc.scalar, "dve": nc.vector, "pool": nc.gpsimd, "sp": nc.sync}
        self.sem = {e: nc.alloc_semaphore("sem_" + e) for e in ["pe", "act", "dve", "pool"]}
        self.semobj = dict(self.sem)
        self.cnt = {e: 0 for e in self.sem}
        self.waited = {e: {} for e in self.eng}
        self.pending = {e: {} for e in self.eng}
        self.dq = {}
        self.dma_last = {}
        self.nops = 0

    def sb(self, stack, name, shape, dt):
        h = stack.enter_context(self.nc.sbuf_tensor(name, list(shape), dt))
        return V(h[:], Buf(name))

    def dram(self, name, shape, dt, kind):
        h = self.nc.dram_tensor(name, list(shape), dt, kind=kind)
        return V(h.ap(), Buf(name)), h

    def _deps(self, reads, writes):
        need = {}
        for v in reads:
            w = v.buf.w
            if w is not None:
                need[w[0]] = max(need.get(w[0], 0), w[1])
        for v in writes:
            w = v.buf.w
            if w is not None:
                need[w[0]] = max(need.get(w[0], 0), w[1])
            for k, val in v.buf.r.items():
                need[k] = max(need.get(k, 0), val)
        return need

    def _wait(self, E, need, defer_one=False):
        for k, val in self.pending[E].items():
            need[k] = max(need.get(k, 0), val)
        self.pending[E] = {}
        todo = []
        for k, val in need.items():
            if k == E and (E == "pe" or not SAME_ENGINE_WAITS):
                continue
            if self.waited[E].get(k, 0) >= val:
                continue
            todo.append((k, val))
            self.waited[E][k] = val
        inline = None
        if defer_one and todo and INLINE_WAITS:
            inline = todo.pop()
        for k, val in todo:
            self.eng[E].wait_ge(self.semobj[k], val)
        return inline

    def _mark(self, tok, reads, writes):
        for v in writes:
            v.buf.w = tok
            v.buf.r = {}
        for v in reads:
            r = v.buf.r
            r[tok[0]] = max(r.get(tok[0], 0), tok[1])

    def op(self, E, fn, reads, writes, inline_ok=True):
        if E != "pe":
            ex = [v for v in reads if v.buf.excl]
            if ex:
                writes = list(writes) + ex
        need = self._deps(reads, writes)
        inline = self._wait(E, need, defer_one=(inline_ok and E != "pe"))
        ins = fn(self.eng[E])
        if inline is not None:
            ins._wait_ge(self.semobj[inline[0]], inline[1])
        self.cnt[E] += 1
        ins.then_inc(self.sem[E], 1)
        self._mark((E, self.cnt[E]), reads, writes)
        self.nops += 1
        return ins

    def dma(self, Q, out, in_, **kw):
        need = self._deps([in_], [out])
        st = self.dq.get(Q)
        if st is None:
            st = {"sems": [self.nc.alloc_semaphore(f"dsem_{Q}_{i}") for i in range(self.NDS)],
                  "uses": [0] * self.NDS, "next": 0}
            self.dq[Q] = st
            for i, sm in enumerate(st["sems"]):
                self.semobj[("d", Q, i)] = sm
        k = st["next"]
        st["next"] = (k + 1) % self.NDS
        key = ("d", Q, k)
        if st["uses"][k] > 0:
            need[key] = max(need.get(key, 0), 16 * st["uses"][k])
        self._wait(Q, need)
        self.eng[Q].dma_start(out=out.ap, in_=in_.ap, **kw).then_inc(st["sems"][k], 16)
        st["uses"][k] += 1
        tok = (key, 16 * st["uses"][k])
        self.dma_last[key] = tok[1]
        self._mark(tok, [in_], [out])
        self.nops += 1

    def allgather_issue(self, out, in_, groups):
        need = self._deps([in_], [out])
        self._wait("pool", need)
        cc_sem = self.nc.alloc_semaphore("cc_sem")
        self.nc.gpsimd.collective_compute("AllGather", ALU.bypass, replica_groups=groups, ins=[in_.ap], outs=[out.ap]).then_inc(cc_sem)
        return cc_sem

    def allgather_wait(self, cc_sem, out, in_, dummy):
        self.nc.gpsimd.wait_ge(cc_sem, 1)
        self.op("pool", lambda e: e.memset(dummy.ap, 0.0), [in_], [dummy, out])

    def barrier(self):
        allt = {}
        for e, c in self.cnt.items():
            if c > 0:
                allt[e] = c
        for k, val in self.dma_last.items():
            allt[k] = val
        for E in self.eng:
            for k, val in allt.items():
                self.pending[E][k] = max(self.pending[E].get(k, 0), val)

    def finish(self):
        need = {}
        for k, val in self.dma_last.items():
            need[k] = val
        for e, c in self.cnt.items():
            if c > 0:
                need[e] = c
        self._wait("sp", need)

    def mm(self, out, lhsT, rhs, start=True, stop=True):
        self.op("pe", lambda e: e.matmul(out.ap, lhsT.ap, rhs.ap, start=start, stop=stop), [lhsT, rhs], [out])

    def tr(self, out, in_, ident):
        self.op("pe", lambda e: e.transpose(out.ap, in_.ap, ident.ap), [in_, ident], [out])

    def act(self, out, in_, func, bias=0.0, scale=1.0, accum=None):
        reads = [in_]
        b = bias
        sc = scale
        if isinstance(bias, V):
            reads.append(bias)
            b = bias.ap
        if isinstance(scale, V):
            reads.append(scale)
            sc = scale.ap
        writes = [out]
        kw = {}
        if accum is not None:
            writes.append(accum)
            kw["accum_out"] = accum.ap
        self.op("act", lambda e: e.activation(out=out.ap, in_=in_.ap, func=func, bias=b, scale=sc, **kw), reads, writes, inline_ok=(accum is None))

    def ts(self, E, out, in0, s1, s2, op0, op1=None):
        reads = [in0]
        a1, a2 = s1, s2
        if isinstance(s1, V):
            reads.append(s1)
            a1 = s1.ap
        if isinstance(s2, V):
            reads.append(s2)
            a2 = s2.ap
        if op1 is None:
            self.op(E, lambda e: e.tensor_scalar(out=out.ap, in0=in0.ap, scalar1=a1, scalar2=None, op0=op0), reads, [out])
        else:
            self.op(E, lambda e: e.tensor_scalar(out=out.ap, in0=in0.ap, scalar1=a1, scalar2=a2, op0=op0, op1=op1), reads, [out])

    def tt(self, E, out, in0, in1, op):
        self.op(E, lambda e: e.tensor_tensor(out=out.ap, in0=in0.ap, in1=in1.ap, op=op), [in0, in1], [out])

    def stt(self, out, in0, scalar, in1, op0, op1):
        reads = [in0, in1]
        sc = scalar
        if isinstance(scalar, V):
            reads.append(scalar)
            sc = scalar.ap
        self.op("dve", lambda e: e.scalar_tensor_tensor(out=out.ap, in0=in0.ap, scalar=sc, in1=in1.ap, op0=op0, op1=op1), reads, [out])

    def cp(self, E, out, in_):
        if E == "act":
            self.op("act", lambda e: e.activation(out=out.ap, in_=in_.ap, func=AF.Copy), [in_], [out])
        else:
            self.op(E, lambda e: e.tensor_copy(out=out.ap, in_=in_.ap), [in_], [out])

    def red(self, out, in_, op=ALU.add):
        self.op("dve", lambda e: e.tensor_reduce(out=out.ap, in_=in_.ap, axis=AX.X, op=op), [in_], [out])

    def scan(self, out, d0, d1, init, op0=ALU.mult, op1=ALU.add):
        reads = [d0, d1]
        ini = init
        if isinstance(init, V):
            reads.append(init)
            ini = init.ap
        self.op("dve", lambda e: e.tensor_tensor_scan(out=out.ap, data0=d0.ap, data1=d1.ap, initial=ini, op0=op0, op1=op1), reads, [out])

    def recip(self, out, in_):
        self.op("dve", lambda e: e.reciprocal(out=out.ap, in_=in_.ap), [in_], [out])

    def memset(self, E, out, val):
        self.op(E, lambda e: e.memset(out.ap, val), [], [out])

    def asel(self, out, in_, pattern, cmp, fill, base, cm):
        self.op("pool", lambda e: e.affine_select(out=out.ap, in_=in_.ap, pattern=pattern, compare_op=cmp, fill=fill,
                                                  base=base, channel_multiplier=cm), [in_], [out])


def build(NT, dbg=None, do_exchange=True, n_cores=8, stop_after=None):
    nc = bass.Bass("TRN2", target_bir_lowering=False)
    kb = KB(nc)
    dbg_outs = {}

    def din(name, shape):
        v, h = kb.dram(name, shape, F32, "ExternalInput")
        return v, h

    x_d, _ = din("x", [NT * TOK, D])
    xprev_d, _ = din("xprev", [1, D])
    fmask_d, fmask_h = din("fmask", [8])
    norm1_d, norm1_h = din("norm1_g", [D])
    w_in_d, _ = din("w_in", [D, D_IN])
    lam_re_d, lam_re_h = din("lam_re", [32, 64])
    lam_im_d, lam_im_h = din("lam_im", [32, 64])
    log_dt_d, log_dt_h = din("log_dt", [32])
    b_re_d, _ = din("b_re", [32, 64, 16])
    b_im_d, _ = din("b_im", [32, 64, 16])
    c_re_d, _ = din("c_re", [32, 16, 64])
    c_im_d, _ = din("c_im", [32, 16, 64])
    d_skip_d, d_skip_h = din("d_skip", [D_SSM])
    w_glu_d, _ = din("w_glu", [D_SSM, D_SSM])
    b_glu_d, b_glu_h = din("b_glu", [D_SSM])
    mu_d, mu_h = din("mu_rwkv", [D_RIN])
    w0_d, w0_h = din("w0", [D_RWKV])
    w2_d, _ = din("w2", [64, D_RWKV])
    a0_d, a0_h = din("a0", [D_RWKV])
    a2_d, _ = din("a2", [64, D_RWKV])
    g2_d, _ = din("g2", [128, D_RWKV])
    k_k_d, k_k_h = din("k_k", [D_RWKV])
    k_a_d, k_a_h = din("k_a", [D_RWKV])
    r_k_d, r_k_h = din("r_k", [D_RWKV])
    lnx_g_d, lnx_g_h = din("lnx_g", [D_RWKV])
    lnx_b_d, lnx_b_h = din("lnx_b", [D_RWKV])
    w_br_d, _ = din("w_branch", [D, D])
    w_out_d, _ = din("w_out", [D, D])
    norm2_d, norm2_h = din("norm2_g", [D])
    w_ff1_d, _ = din("w_ff1", [D, D_FF])
    w_ff2_d, _ = din("w_ff2", [D_FF, D])
    normf_d, normf_h = din("norm_f_g", [D])
    out_d, _ = kb.dram("out", [NT * TOK, D], F32, "ExternalOutput")

    rec_yT, _ = kb.dram("rec_yT", [NT, 128, 512], F32, "Internal")
    rec_Y, _ = kb.dram("rec_Y", [NT, 128, 512], F32, "Internal")
    rec_ZT, _ = kb.dram("rec_ZT", [NT, 64, 1024], BF16, "Internal")
    rec_g, _ = kb.dram("rec_g", [NT, 128, 512], BF16, "Internal")
    rec_bv, _ = kb.dram("rec_bv", [NT, 128, 512], BF16, "Internal")
    rec_x1, _ = kb.dram("rec_x1", [NT, 128, D], F32, "Internal")
    summ_d, _ = kb.dram("summ_d", [128, 1056], F32, "Internal")
    gath_d, _ = kb.dram("gath_d", [4 * 128, 1056], F32, "Internal")

    def bcast(h, n, off=0):
        return V(bass.AP(tensor=h, offset=off, ap=[[0, 128], [1, n]]), Buf("bc"))

    def dbg_out(name, v, shape, dt=F32):
        if dbg is None or name not in dbg:
            return
        o, _ = kb.dram("dbg_" + name, shape, dt, "ExternalOutput")
        kb.dma("sp", o, v)
        dbg_outs[name] = shape

    root = ExitStack()
    with root:
        psb = []
        for i in range(8):
            h = root.enter_context(nc.psum_tensor(f"psb{i}", [128, 512], F32))
            psb.append(V(h[:], Buf(f"psb{i}", excl=True)))
        ps_rr = [0]

        def ps():
            b = psb[1 + ps_rr[0]]
            ps_rr[0] = (ps_rr[0] + 1) % 6
            return b

        ps_long = psb[7]

        G = root
        ones_f = kb.sb(G, "ones_f", [128, 128], F32)
        ident_f = kb.sb(G, "ident_f", [128, 128], F32)
        ident_b = kb.sb(G, "ident_b", [128, 128], BF16)
        kb.memset("pool", ones_f, 1.0)
        kb.asel(ident_f, ones_f, [[-1, 128]], ALU.is_equal, 0.0, 0, 1)
        kb.cp("pool", ident_b, ident_f)
        g1_bc = kb.sb(G, "g1_bc", [128, D], F32)
        kb.dma("sp", g1_bc, bcast(norm1_h, D))
        eps_c = kb.sb(G, "eps_c", [128, 1], F32)
        kb.memset("pool", eps_c, RMS_EPS)

        def norm_tile(xt, rows, gbc, hb, scr, ss, rstd):
            junk = scr if scr is not None else hb
            kb.act(junk[0:rows, :], xt[0:rows, :], AF.Square, accum=ss[0:rows, :])
            kb.act(ss[0:rows, :], ss[0:rows, :], AF.Sqrt, bias=eps_c[0:rows, :], scale=1.0 / D)
            kb.recip(rstd[0:rows, :], ss[0:rows, :])
            kb.stt(hb[0:rows, :], xt[0:rows, :], rstd[0:rows, :], gbc[0:rows, :], ALU.mult, ALU.mult)

        def transpose_rows(dstT, src_b, rows, nchunk):
            for c0 in range(0, nchunk, 8):
                p = ps()
                pb = p.bitcast(BF16)
                n = min(8, nchunk - c0)
                if rows == 128:
                    for c in range(n):
                        kb.tr(pb[:, c * 128:(c + 1) * 128], src_b[:, (c0 + c) * 128:(c0 + c + 1) * 128], ident_b)
                    kb.cp("act", dstT[:, c0:c0 + n, :], V(pb.ap[:, 0:n * 128].rearrange("p (c t) -> p c t", c=n), p.buf))
                else:
                    assert rows == 1
                    for c in range(n):
                        kb.tr(pb[:, 2 * c:2 * c + 1], src_b[0:1, (c0 + c) * 128:(c0 + c + 1) * 128], ident_b[0:1, 0:1])
                    kb.cp("act", dstT[:, c0:c0 + n, :], V(pb.ap[:, 0:2 * n].rearrange("p (c t) -> p c t", c=n)[:, :, 0:1], p.buf))

        COS = kb.sb(G, "COS", [128, 16, 128], BF16)
        SIN = kb.sb(G, "SIN", [128, 16, 128], BF16)
        FM = kb.sb(G, "FM", [128, 8], F32)
        CRE = kb.sb(G, "CRE", [128, 16, 128], BF16)
        NCIM = kb.sb(G, "NCIM", [128, 16, 128], BF16)
        RHO = kb.sb(G, "RHO", [128, 16], F32)
        PHI = kb.sb(G, "PHI", [128, 16], F32)
        LNA = kb.sb(G, "LNA", [128, 16], F32)
        ABRE = kb.sb(G, "ABRE", [128, 16], F32)
        ABIM = kb.sb(G, "ABIM", [128, 16], F32)
        R128C = kb.sb(G, "R128C", [128, 16], F32)
        R128S = kb.sb(G, "R128S", [128, 16], F32)
        DSK = kb.sb(G, "DSK", [128, 4], F32)
        TP1 = kb.sb(G, "TP1", [128, 128], F32)
        XIN_RE = kb.sb(G, "XIN_RE", [128, 16], F32)
        XIN_IM = kb.sb(G, "XIN_IM", [128, 16], F32)
        ST = [kb.sb(G, f"ST{i}", [64, 4, 64], F32) for i in range(2)]
        STb = [kb.sb(G, f"STb{i}", [64, 4, 64], BF16) for i in range(2)]
        MT = [kb.sb(G, f"MT{i}", [64, 4, 64], F32) for i in range(2)]
        STIN = kb.sb(G, "STIN", [64, 8, 64], BF16)
        XC_RE = kb.sb(G, "XC_RE", [128, 16], F32)
        XC_IM = kb.sb(G, "XC_IM", [128, 16], F32)
        gn_eps = kb.sb(G, "gn_eps", [128, 1], F32)

        with ExitStack() as A:
            WIN = kb.sb(A, "WIN", [128, KC, 2304], BF16)
            WSH = kb.sb(A, "WSH", [128, KC, D_RIN], BF16)
            BRE = kb.sb(A, "BRE", [128, 16, 128], BF16)
            BIM = kb.sb(A, "BIM", [128, 16, 128], BF16)
            WA2 = kb.sb(A, "WA2", [128, D_RWKV], BF16)
            G2B = kb.sb(A, "G2B", [128, D_RWKV], BF16)
            w0_bc = kb.sb(A, "w0_bc", [128, 512], F32)
            a0_bc = kb.sb(A, "a0_bc", [128, 512], F32)
            kk_bc = kb.sb(A, "kk_bc", [128, 512], F32)
            ka_bc = kb.sb(A, "ka_bc", [128, 512], F32)
            rk_bc = kb.sb(A, "rk_bc", [128, 512], F32)
            MASK4 = kb.sb(A, "MASK4", [128, 512], BF16)
            ML = kb.sb(A, "ML", [128, 128], BF16)
            TRI = kb.sb(A, "TRI", [128, 4, 128], F32)
            I64R = kb.sb(A, "I64R", [64, 64], BF16)

            with ExitStack() as P0:
                negc = kb.sb(P0, "negc", [128, 128], F32)
                kb.memset("pool", negc, NEG_EH)
                kb.asel(MASK4[:, 0:128], ones_f, [[1, 128]], ALU.is_gt, 0.0, 0, -1)
                kb.asel(MASK4[:, 128:256], ones_f, [[1, 128]], ALU.is_ge, 0.0, 0, -1)
                kb.cp("pool", MASK4[:, 256:512], MASK4[:, 0:256])
                kb.asel(ML, ones_f, [[-1, 128]], ALU.is_gt, 0.0, 0, 1)
                kb.asel(TRI[:, 0, :], negc, [[1, 128]], ALU.is_ge, 0.0, 0, -1)
                kb.asel(TRI[:, 1, :], negc, [[1, 128]], ALU.is_gt, 0.0, 0, -1)
                kb.asel(TRI[:, 2, :], negc, [[-1, 128]], ALU.is_gt, 0.0, 0, 1)
                kb.cp("pool", TRI[:, 3, :], negc)
                kb.cp("pool", I64R, ident_f[0:64, 0:64])
                kb.dma("pool", WA2[0:64, :], w2_d)
                kb.dma("pool", WA2[64:128, :], a2_d)
                kb.dma("pool", G2B, g2_d)
                for t_, h_ in ((w0_bc, w0_h), (a0_bc, a0_h), (kk_bc, k_k_h), (ka_bc, k_a_h), (rk_bc, r_k_h)):
                    kb.dma("sp", t_, bcast(h_, 512))
                kb.dma("pool", WIN[:, :, 0:512], V(w_in_d.ap[:, 0:512].rearrange("(kc p) c -> p kc c", p=128), w_in_d.buf))
                mu_bc = kb.sb(P0, "mu_bc", [128, D_RIN], F32)
                kb.dma("sp", mu_bc, bcast(mu_h, D_RIN))
                omm_bc = kb.sb(P0, "omm_bc", [128, D_RIN], F32)
                kb.ts("dve", omm_bc, mu_bc, -1.0, 1.0, ALU.mult, ALU.add)
                stg = [kb.sb(P0, f"stg{i}", [128, D_RIN], F32) for i in range(2)]
                for kc in range(KC):
                    s_ = stg[kc % 2]
                    kb.dma("sp", s_, w_in_d[kc * 128:(kc + 1) * 128, 512:2304])
                    kb.tt("dve", WSH[:, kc, :], s_, mu_bc, ALU.mult)
                    kb.tt("pool", WIN[:, kc, 512:2304], s_, omm_bc, ALU.mult)

                LR = kb.sb(P0, "LR", [128, 16], F32)
                LI = kb.sb(P0, "LI", [128, 16], F32)
                DT = kb.sb(P0, "DT", [128, 16], F32)
                kb.dma("sp", LR, V(bass.AP(tensor=lam_re_h, offset=0, ap=[[1, 128], [128, 16]]), lam_re_d.buf), allow_slow_non_contiguous=True)
                kb.dma("sp", LI, V(bass.AP(tensor=lam_im_h, offset=0, ap=[[1, 128], [128, 16]]), lam_im_d.buf), allow_slow_non_contiguous=True)
                for gl in range(2):
                    kb.dma("sp", DT[gl * 64:(gl + 1) * 64, :], V(bass.AP(tensor=log_dt_h, offset=gl, ap=[[0, 64], [2, 16]]), log_dt_d.buf),
                           allow_slow_non_contiguous=True)
                kb.dma("sp", DSK, V(bass.AP(tensor=d_skip_h, offset=0, ap=[[1, 128], [128, 4]]), d_skip_d.buf), allow_slow_non_contiguous=True)
                kb.act(DT, DT, AF.Exp)
                kb.ts("dve", LR, LR, -1e-4, None, ALU.min)
                kb.tt("dve", LNA, LR, DT, ALU.mult)
                kb.act(RHO, LNA, AF.Exp)
                TH = kb.sb(P0, "TH", [128, 16], F32)
                kb.tt("dve", TH, LI, DT, ALU.mult)
                sI = kb.sb(P0, "sI", [128, 128], I32)
                sF = kb.sb(P0, "sF", [128, 128], F32)
                sA = kb.sb(P0, "sA", [128, 128], F32)
                sB = kb.sb(P0, "sB", [128, 128], F32)

                def reduce_angle(dst, src, n, shift):
                    kb.ts("dve", sF[:, 0:n], src, 1.0 / TWO_PI, None, ALU.mult)
                    kb.cp("pool", sI[:, 0:n], sF[:, 0:n])
                    kb.cp("pool", sF[:, 0:n], sI[:, 0:n])
                    kb.stt(dst, sF[:, 0:n], -TWO_PI, src, ALU.mult, ALU.add)
                    if shift != 0.0:
                        kb.ts("dve", dst, dst, shift, None, ALU.add)
                    for _ in range(1):
                        kb.ts("dve", sF[:, 0:n], dst, -math.pi, TWO_PI, ALU.is_lt, ALU.mult)
                        kb.tt("dve", dst, dst, sF[:, 0:n], ALU.add)
                        kb.ts("dve", sF[:, 0:n], dst, math.pi, -TWO_PI, ALU.is_gt, ALU.mult)
                        kb.tt("dve", dst, dst, sF[:, 0:n], ALU.add)

                reduce_angle(PHI, TH, 16, 0.0)
                CP = kb.sb(P0, "CP", [128, 16], F32)
                SP_ = kb.sb(P0, "SP_", [128, 16], F32)
                kb.act(SP_, PHI, AF.Sin)
                reduce_angle(sA[:, 0:16], PHI, 16, math.pi / 2)
                kb.act(CP, sA[:, 0:16], AF.Sin)
                kb.tt("dve", ABRE, RHO, CP, ALU.mult)
                kb.tt("dve", ABIM, RHO, SP_, ALU.mult)
                DEN = kb.sb(P0, "DEN", [128, 16], F32)
                T1 = kb.sb(P0, "T1", [128, 16], F32)
                T2 = kb.sb(P0, "T2", [128, 16], F32)
                XM1 = kb.sb(P0, "XM1", [128, 16], F32)
                QRE = kb.sb(P0, "QRE", [128, 16], F32)
                QIM = kb.sb(P0, "QIM", [128, 16], F32)
                NQIM = kb.sb(P0, "NQIM", [128, 16], F32)
                kb.tt("dve", DEN, LR, LR, ALU.mult)
                kb.tt("dve", T1, LI, LI, ALU.mult)
                kb.tt("dve", DEN, DEN, T1, ALU.add)
                kb.recip(DEN, DEN)
                kb.ts("dve", XM1, ABRE, -1.0, None, ALU.add)
                kb.tt("dve", T1, XM1, LR, ALU.mult)
                kb.tt("dve", T2, ABIM, LI, ALU.mult)
                kb.tt("dve", T1, T1, T2, ALU.add)
                kb.tt("dve", QRE, T1, DEN, ALU.mult)
                kb.tt("dve", T1, ABIM, LR, ALU.mult)
                kb.tt("dve", T2, XM1, LI, ALU.mult)
                kb.tt("dve", T1, T1, T2, ALU.subtract)
                kb.tt("dve", QIM, T1, DEN, ALU.mult)
                kb.ts("dve", NQIM, QIM, -1.0, None, ALU.mult)
                kb.op("pool", lambda e: e.iota(sI.ap, pattern=[[1, 128]], base=1, channel_multiplier=0), [], [sI])
                kb.cp("pool", TP1, sI)
                kb.ts("dve", T1, PHI, 128.0, None, ALU.mult)
                reduce_angle(T2, T1, 16, 0.0)
                kb.act(R128S, T2, AF.Sin)
                reduce_angle(T2, T1, 16, math.pi / 2)
                kb.act(R128C, T2, AF.Sin)
                bA = kb.sb(P0, "bA", [128, 2048], F32)
                bB = kb.sb(P0, "bB", [128, 2048], F32)
                bF = kb.sb(P0, "bF", [128, 2048], F32)
                bI = kb.sb(P0, "bI", [128, 2048], I32)
                kb.tt("dve", V(bA.ap.rearrange("p (j t) -> p j t", j=16), bA.buf),
                      V(PHI.ap.unsqueeze(2).broadcast_to([128, 16, 128]), PHI.buf),
                      V(TP1.ap.unsqueeze(1).broadcast_to([128, 16, 128]), TP1.buf), ALU.mult)

                def reduce_angle_big(dst, src, shift):
                    kb.ts("dve", bF, src, 1.0 / TWO_PI, None, ALU.mult)
                    kb.cp("pool", bI, bF)
                    kb.cp("pool", bF, bI)
                    kb.stt(dst, bF, -TWO_PI, src, ALU.mult, ALU.add)
                    if shift != 0.0:
                        kb.ts("dve", dst, dst, shift, None, ALU.add)
                    for _ in range(1):
                        kb.ts("dve", bF, dst, -math.pi, TWO_PI, ALU.is_lt, ALU.mult)
                        kb.tt("dve", dst, dst, bF, ALU.add)
                        kb.ts("dve", bF, dst, math.pi, -TWO_PI, ALU.is_gt, ALU.mult)
                        kb.tt("dve", dst, dst, bF, ALU.add)

                reduce_angle_big(bB, bA, 0.0)
                kb.act(V(SIN.ap.rearrange("p j t -> p (j t)"), SIN.buf), bB, AF.Sin)
                reduce_angle_big(bB, bA, math.pi / 2)
                kb.act(V(COS.ap.rearrange("p j t -> p (j t)"), COS.buf), bB, AF.Sin)
                BXR = kb.sb(P0, "BXR", [128, 16, 128], F32)
                BXI = kb.sb(P0, "BXI", [128, 16, 128], F32)
                kb.memset("pool", BXR, 0.0)
                kb.memset("pool", BXI, 0.0)
                for j in range(16):
                    for gl in range(2):
                        c0 = (j % 4) * 32 + gl * 16
                        g = 2 * j + gl
                        kb.dma("sp", BXR[gl * 64:(gl + 1) * 64, j, c0:c0 + 16], b_re_d[g, :, :])
                        kb.dma("sp", BXI[gl * 64:(gl + 1) * 64, j, c0:c0 + 16], b_im_d[g, :, :])
                btR = kb.sb(P0, "btR", [128, 16, 128], BF16)
                btI = kb.sb(P0, "btI", [128, 16, 128], BF16)
                QR3 = V(QRE.ap.unsqueeze(2).broadcast_to([128, 16, 128]), QRE.buf)
                QI3 = V(QIM.ap.unsqueeze(2).broadcast_to([128, 16, 128]), QIM.buf)
                bA3 = V(bA.ap.rearrange("p (j t) -> p j t", j=16), bA.buf)
                bB3 = V(bB.ap.rearrange("p (j t) -> p j t", j=16), bB.buf)
                kb.tt("dve", bA3, BXI, QI3, ALU.mult)
                kb.tt("pool", bB3, BXR, QR3, ALU.mult)
                kb.tt("dve", btR, bB3, bA3, ALU.subtract)
                kb.tt("dve", bA3, BXR, QI3, ALU.mult)
                kb.tt("pool", bB3, BXI, QR3, ALU.mult)
                kb.tt("dve", btI, bB3, bA3, ALU.add)
                for j in range(16):
                    p = ps()
                    pb = p.bitcast(BF16)
                    kb.tr(pb[:, 0:128], btR[:, j, :], ident_b)
                    kb.tr(pb[:, 128:256], btI[:, j, :], ident_b)
                    kb.cp("act", BRE[:, j, :], pb[:, 0:128])
                    kb.cp("act", BIM[:, j, :], pb[:, 128:256])
                kb.memset("pool", CRE, 0.0)
                kb.memset("pool", NCIM, 0.0)
                CXR = kb.sb(P0, "CXR", [128, 128], F32)
                CXI = kb.sb(P0, "CXI", [128, 128], F32)
                for cb in range(4):
                    kb.memset("pool", CXR, 0.0)
                    kb.memset("pool", CXI, 0.0)
                    for g8 in range(8):
                        g = cb * 8 + g8
                        c0 = (g8 % 2) * 64
                        kb.dma("sp", CXR[g8 * 16:(g8 + 1) * 16, c0:c0 + 64], c_re_d[g, :, :])
                        kb.dma("sp", CXI[g8 * 16:(g8 + 1) * 16, c0:c0 + 64], c_im_d[g, :, :])
                    p = ps()
                    kb.tr(p[:, 0:128], CXR, ident_f)
                    kb.tr(p[:, 128:256], CXI, ident_f)
                    for jj in range(4):
                        j = cb * 4 + jj
                        kb.cp("act", CRE[:, j, jj * 32:(jj + 1) * 32], p[:, jj * 32:(jj + 1) * 32])
                        kb.act(NCIM[:, j, jj * 32:(jj + 1) * 32], p[:, 128 + jj * 32:128 + (jj + 1) * 32], AF.Copy, scale=-1.0)
            kb.barrier()
            if stop_after == "P0":
                kb.finish()
                return nc, dbg_outs

            xt = kb.sb(A, "xt", [128, D], F32)
            hb = kb.sb(A, "hb", [128, D], BF16)
            scr = None
            ss = kb.sb(A, "ss", [128, 1], F32)
            rstd = kb.sb(A, "rstd", [128, 1], F32)
            hTx = [kb.sb(A, f"hTx{i}", [128, KC, 130], BF16) for i in range(2)]
            hP = kb.sb(A, "hP", [128, KC, 1], BF16)
            uT = kb.sb(A, "uT", [128, 4, 128], BF16)
            WL_RE = kb.sb(A, "WL_RE", [128, 16], F32)
            WL_IM = kb.sb(A, "WL_IM", [128, 16], F32)
            kb.memset("pool", XIN_RE, 0.0)
            kb.memset("pool", XIN_IM, 0.0)
            s5t = [kb.sb(A, f"s5t{i}", [128, 128], F32) for i in range(4)]
            s5z = [[kb.sb(A, f"s5z{i}_{q}", [128, 128], BF16) for q in range(4)] for i in range(3)]
            s5w = [(kb.sb(A, f"s5wr{i}", [128, 128], F32), kb.sb(A, f"s5wi{i}", [128, 128], F32)) for i in range(2)]
            s5c = [kb.sb(A, f"s5c{i}", [128, 16], F32) for i in range(2)]
            yT = kb.sb(A, "yT", [128, 2, 128], F32)
            txa = kb.sb(A, "txa", [128, 128], BF16)
            sxg = kb.sb(A, "sxg", [128, 128], BF16)
            sgw = kb.sb(A, "sgw", [128, 512], F32)
            asig = kb.sb(A, "asig", [128, 512], F32)
            g_sb = kb.sb(A, "g_sb", [128, 512], BF16)
            eW = kb.sb(A, "eW", [128, 512], BF16)
            eWi = kb.sb(A, "eWi", [128, 512], BF16)
            eWx = kb.sb(A, "eWx", [128, 512], BF16)
            eRem = kb.sb(A, "eRem", [128, 512], BF16)
            kkn = kb.sb(A, "kkn", [128, 512], F32)
            kmod = sgw
            bvec = kb.sb(A, "bvec", [128, 512], F32)
            tmpA = kb.sb(A, "tmpA", [128, 512], F32)
            st8 = [kb.sb(A, f"st8_{i}", [128, 8], F32) for i in range(3)]
            eTot = [kb.sb(A, f"eTot{i}", [64, 512], F32) for i in range(2)]
            At = [kb.sb(A, f"At{i}", [128, 512], BF16) for i in range(2)]
            Rt = [kb.sb(A, f"Rt{i}", [128, 512], BF16) for i in range(2)]
            Btl = [kb.sb(A, f"Btl{i}", [128, 512], BF16) for i in range(2)]
            Ktl = [kb.sb(A, f"Ktl{i}", [128, 512], BF16) for i in range(2)]
            Vb = [kb.sb(A, f"Vb{i}", [128, 512], BF16) for i in range(2)]
            Bh = [kb.sb(A, f"Bh{i}", [128, 512], BF16) for i in range(2)]
            Kh = [kb.sb(A, f"Kh{i}", [128, 512], BF16) for i in range(2)]
            bv = g_sb
            HGB = []
            for hg_ in range(2):
                HGB.append((
                    kb.sb(A, f"KM{hg_}", [64, 4, 512], BF16),
                    [kb.sb(A, f"S1_{hg_}_{h}", [128, 512], BF16) for h in range(4)],
                    [kb.sb(A, f"NTp{hg_}_{i}", [128, 4, 128], BF16) for i in range(2)],
                    [kb.sb(A, f"Np{hg_}_{i}", [128, 4, 128], BF16) for i in range(2)],
                    kb.sb(A, f"Gm{hg_}", [128, 4, 128], BF16),
                    kb.sb(A, f"U1_{hg_}", [128, 4, 64], BF16),
                    kb.sb(A, f"AhU{hg_}", [128, 4, 128], BF16),
                    kb.sb(A, f"RhT{hg_}", [64, 4, 128], BF16),
                    kb.sb(A, f"Pm{hg_}", [64, 4, 64], F32),
                ))
            MTb = [kb.sb(A, f"MTb{i}", [64, 4, 64], BF16) for i in range(2)]
            ZT = [kb.sb(A, f"ZT{i}", [64, 4, 128], BF16) for i in range(2)]
            Yl = [kb.sb(A, f"Yl{i}", [128, 256], F32) for i in range(2)]
            for hg_ in range(2):
                kb.memset("pool", ST[hg_], 0.0)
                kb.memset("pool", STb[hg_], 0.0)
                for h in range(4):
                    kb.cp("pool", MT[hg_][:, h, :], ident_f[0:64, 0:64])
                    kb.cp("pool", MTb[hg_][:, h, :], ident_f[0:64, 0:64])

            psa_rr = [0]

            def psA():
                b = psb[psa_rr[0]]
                psa_rr[0] = (psa_rr[0] + 1) % 7
                return b

            kb.dma("sp", xt[0:1, :], xprev_d)
            norm_tile(xt, 1, g1_bc, hb, scr, ss, rstd)
            transpose_rows(hP, hb, 1, KC)

            def front_gen(n):
                par = n % 2
                hTc = hTx[par][:, :, 1:129]
                kb.dma("sp", xt, x_d[n * 128:(n + 1) * 128, :])
                norm_tile(xt, 128, g1_bc, hb, scr, ss, rstd)
                yield
                p = psA()
                pb = p.bitcast(BF16)
                for c in range(KC):
                    kb.tr(pb[:, c * 128:(c + 1) * 128], hb[:, c * 128:(c + 1) * 128], ident_b)
                kb.cp("act", hTc, V(pb.ap.rearrange("p (c t) -> p c t", c=KC), p.buf))
                if n == 0:
                    kb.cp("pool", hTx[par][:, :, 0:1], hP)
                else:
                    kb.cp("pool", hTx[par][:, :, 0:1], hTx[1 - par][:, :, 128:129])
                yield
                p = psA()
                for cb in range(4):
                    for kc in range(KC):
                        kb.mm(p[:, cb * 128:(cb + 1) * 128], WIN[:, kc, cb * 128:(cb + 1) * 128], hTc[:, kc, :], start=(kc == 0), stop=(kc == KC - 1))
                kb.cp("act", uT, V(p.ap.rearrange("p (c t) -> p c t", c=4), p.buf))
                yield

            def s5_gen(n):
                pY = ps_long
                t1, t2, t3, t4 = s5t[0:4]

                def emit_out(j):
                    cb, jj = divmod(j, 4)
                    z1, z2, z3, z4 = s5z[j % 3]
                    kb.mm(pY[:, cb * 128:(cb + 1) * 128], CRE[:, j, :], z1, start=(jj == 0), stop=False)
                    kb.mm(pY[:, cb * 128:(cb + 1) * 128], CRE[:, j, :], z2, start=False, stop=False)
                    kb.mm(pY[:, cb * 128:(cb + 1) * 128], NCIM[:, j, :], z3, start=False, stop=False)
                    kb.mm(pY[:, cb * 128:(cb + 1) * 128], NCIM[:, j, :], z4, start=False, stop=(jj == 3))

                for j in range(16):
                    cb = j // 4
                    sl = j % 2
                    wre, wim = s5w[sl]
                    z1, z2, z3, z4 = s5z[j % 3]
                    cosj, sinj = COS[:, j, :], SIN[:, j, :]
                    pB = psA()
                    kb.mm(pB[:, 0:128], BRE[:, j, :], uT[:, cb, :])
                    kb.mm(pB[:, 128:256], BIM[:, j, :], uT[:, cb, :])
                    kb.tt("dve", t1, pB[:, 0:128], cosj, ALU.mult)
                    kb.tt("dve", t2, pB[:, 128:256], sinj, ALU.mult)
                    kb.tt("dve", t3, pB[:, 128:256], cosj, ALU.mult)
                    kb.tt("dve", t4, pB[:, 0:128], sinj, ALU.mult)
                    kb.tt("dve", t1, t1, t2, ALU.add)
                    kb.tt("dve", t3, t3, t4, ALU.subtract)
                    rho_b = V(RHO.ap[:, j:j + 1].broadcast_to([128, 128]), RHO.buf)
                    kb.scan(wre, rho_b, t1, XIN_RE[:, j:j + 1])
                    kb.scan(wim, rho_b, t3, XIN_IM[:, j:j + 1])
                    if j >= 2:
                        emit_out(j - 2)
                    yield
                    kb.cp("act", WL_RE[:, j:j + 1], wre[:, 127:128])
                    kb.cp("act", WL_IM[:, j:j + 1], wim[:, 127:128])
                    kb.tt("pool", z1, wre, cosj, ALU.mult)
                    kb.stt(z2, wim, -1.0, sinj, ALU.mult, ALU.mult)
                    kb.tt("pool", z3, wre, sinj, ALU.mult)
                    kb.tt("pool", z4, wim, cosj, ALU.mult)
                    yield
                emit_out(14)
                emit_out(15)
                c1 = s5c[0]
                c2 = s5c[1]
                kb.tt("pool", c1, WL_RE, R128C, ALU.mult)
                kb.tt("pool", c2, WL_IM, R128S, ALU.mult)
                kb.tt("pool", XIN_RE, c1, c2, ALU.subtract)
                kb.tt("pool", c1, WL_RE, R128S, ALU.mult)
                kb.tt("pool", c2, WL_IM, R128C, ALU.mult)
                kb.tt("pool", XIN_IM, c1, c2, ALU.add)
                for cb in range(4):
                    kb.stt(yT[:, cb % 2, :], uT[:, cb, :], DSK[:, cb:cb + 1], pY[:, cb * 128:(cb + 1) * 128], ALU.mult, ALU.add)
                    if cb % 2 == 1:
                        kb.dma("sp", rec_yT[n][:, (cb - 1) * 128:(cb + 1) * 128], V(yT.ap.rearrange("p c t -> p (c t)"), yT.buf))
                yield

            def prep_gen(n):
                par = n % 2
                hTc = hTx[par][:, :, 1:129]
                hTs = hTx[par][:, :, 0:128]
                pX = psA()
                for (c_w, c_s, o_) in ((2176, 1664, 0), (2048, 1536, 128)):
                    for kc in range(KC):
                        kb.mm(pX[:, o_:o_ + 128], WIN[:, kc, c_w:c_w + 128], hTc[:, kc, :], start=(kc == 0), stop=False)
                    for kc in range(KC):
                        kb.mm(pX[:, o_:o_ + 128], WSH[:, kc, c_s:c_s + 128], hTs[:, kc, :], start=False, stop=(kc == KC - 1))
                kb.act(txa[0:64, :], pX[0:64, 0:128], AF.Tanh)
                kb.cp("act", txa[64:128, :], pX[64:128, 0:128])
                kb.act(sxg, pX[:, 128:256], AF.Sigmoid)
                yield
                pW = psA()
                kb.mm(pW, txa[0:64, :], WA2[0:64, :])
                pA = psA()
                kb.mm(pA, txa[64:128, :], WA2[64:128, :])
                pG = psA()
                kb.mm(pG, sxg, G2B)
                kb.tt("dve", tmpA, pW, w0_bc, ALU.add)
                kb.tt("dve", bvec, pA, a0_bc, ALU.add)
                kb.cp("act", g_sb, pG)
                kb.act(sgw, tmpA, AF.Sigmoid)
                kb.act(asig, bvec, AF.Sigmoid)
                kb.dma("sp", rec_g[n], g_sb)
                yield
                pC = [psA() for _ in range(4)]
                for i in range(4):
                    kb.mm(pC[i], TRI[:, i, :], sgw)
                kb.act(eW, pC[0], AF.Exp)
                kb.act(eWi, pC[0], AF.Exp, scale=-1.0)
                kb.act(eWx, pC[1], AF.Exp)
                kb.act(eRem, pC[2], AF.Exp)
                kb.act(eTot[par], pC[3][0:64, :], AF.Exp)
                kb.tt("pool", V(eTot[par].ap.rearrange("p (h k) -> p h k", h=8), eTot[par].buf), V(eTot[par].ap.rearrange("p (h k) -> p h k", h=8), eTot[par].buf), V(I64R.ap.unsqueeze(1).broadcast_to([64, 8, 64]), I64R.buf), ALU.mult)
                kb.stt(kmod, asig, -1.0, ka_bc, ALU.add, ALU.mult)
                kb.ts("pool", kmod, kmod, 1.0, None, ALU.add)
                yield
                pR, pK, pV = psA(), psA(), psA()
                for (pp, o_) in ((pR, 0), (pK, 512), (pV, 1024)):
                    for kc in range(KC):
                        kb.mm(pp, hTc[:, kc, :], WIN[:, kc, 512 + o_:512 + o_ + 512], start=(kc == 0), stop=False)
                    for kc in range(KC):
                        kb.mm(pp, hTs[:, kc, :], WSH[:, kc, o_:o_ + 512], start=False, stop=(kc == KC - 1))
                kb.cp("act", Vb[par], pV)
                kb.tt("dve", Rt[par], pR, eW, ALU.mult)
                kb.tt("dve", tmpA, pR, rk_bc, ALU.mult)
                kb.tt("dve", kkn, pK, kk_bc, ALU.mult)
                kb.tt("dve", kmod, pK, kmod, ALU.mult)
                yield
                kb.tt("pool", bvec, kkn, kkn, ALU.mult)
                kb.red(st8[0], V(bvec.ap.rearrange("p (h k) -> p h k", h=8), bvec.buf))
                kb.act(st8[0], st8[0], AF.Sqrt)
                kb.ts("dve", st8[0], st8[0], 1e-12, None, ALU.max)
                kb.recip(st8[0], st8[0])
                yield
                kk3 = V(kkn.ap.rearrange("p (h k) -> p h k", h=8), kkn.buf)
                kb.tt("pool", kk3, kk3, V(st8[0].ap.unsqueeze(2).broadcast_to([128, 8, 64]), st8[0].buf), ALU.mult)
                kb.tt("pool", bvec, kkn, asig, ALU.mult)
                yield
                kb.tt("dve", tmpA, tmpA, kmod, ALU.mult)
                kb.red(st8[1], V(tmpA.ap.rearrange("p (h k) -> p h k", h=8), tmpA.buf))
                kb.tt("pool", V(bv.ap.rearrange("p (h k) -> p h k", h=8), bv.buf), V(Vb[par].ap.rearrange("p (h k) -> p h k", h=8), Vb[par].buf),
                      V(st8[1].ap.unsqueeze(2).broadcast_to([128, 8, 64]), st8[1].buf), ALU.mult)
                kb.dma("sp", rec_bv[n], bv)
                yield
                kb.tt("dve", Ktl[par], kmod, eWi, ALU.mult)
                kb.tt("pool", Btl[par], bvec, eWi, ALU.mult)
                kb.stt(At[par], kkn, -1.0, eWx, ALU.mult, ALU.mult)
                kb.tt("pool", Kh[par], kmod, eRem, ALU.mult)
                kb.tt("dve", Bh[par], bvec, eRem, ALU.mult)
                yield

            def fsp_gen(n):
                yield from front_gen(n)
                g5, gp = s5_gen(n), prep_gen(n)
                alive = [g5, gp]
                while alive:
                    for g_ in list(alive):
                        try:
                            next(g_)
                            yield
                        except StopIteration:
                            alive.remove(g_)

            def rw_gen(n, hg):
                par = n % 2
                KM, S1, NTp, Np, Gm, U1, AhU, RhT, Pm = HGB[hg]
                At_, Rt_, Btl_, Ktl_, Vb_, Bh_, Kh_, eTot_ = At[par], Rt[par], Btl[par], Ktl[par], Vb[par], Bh[par], Kh[par], eTot[par]
                hs = [hg * 4 + i for i in range(4)]
                for pair in range(2):
                    p = psA()
                    pb = p.bitcast(BF16)
                    for i in range(2):
                        hl = pair * 2 + i
                        h = hs[hl]
                        for q, src in enumerate((At_, Rt_, Btl_, Ktl_)):
                            kb.tr(pb[0:64, (i * 4 + q) * 128:(i * 4 + q + 1) * 128], src[:, h * 64:(h + 1) * 64], ident_b)
                    kb.cp("act", KM[:, pair * 2:pair * 2 + 2, :], V(pb.ap[0:64, :].rearrange("p (h c) -> p h c", h=2), p.buf))
                    yield
                pY1 = psA()
                for hl in range(4):
                    p = psA()
                    kb.mm(p[:, 0:256], KM[:, hl, 256:384], KM[:, hl, 0:256])
                    kb.mm(p[:, 256:512], KM[:, hl, 384:512], KM[:, hl, 0:256])
                    kb.mm(pY1[:, hl * 128:(hl + 1) * 128], KM[:, hl, 0:128], KM[:, hl, 256:384])
                    kb.tt("dve", S1[hl], p, MASK4, ALU.mult)
                kb.tt("dve", NTp[0], V(pY1.ap.rearrange("p (h t) -> p h t", h=4), pY1.buf), V(ML.ap.unsqueeze(1).broadcast_to([128, 4, 128]), ML.buf), ALU.mult)
                yield
                p = psA()
                for hl in range(4):
                    h = hs[hl]
                    kb.mm(p[:, hl * 64:(hl + 1) * 64], S1[hl][:, 256:384], Vb_[:, h * 64:(h + 1) * 64])
                kb.cp("act", U1, V(p.ap[:, 0:256].rearrange("p (h v) -> p h v", h=4), p.buf))
                for hl in range(4):
                    kb.tt("pool", Gm[:, hl, :], S1[hl][:, 0:128], ident_b, ALU.add)
                yield
                cur = 0
                for lev in range(1, 8):
                    last = (lev == 7)
                    Ncur = [(S1[hl][:, 0:128] if lev == 1 else Np[cur][:, hl, :]) for hl in range(4)]
                    NTcur = [NTp[cur][:, hl, :] for hl in range(4)]
                    nxt = 1 - cur
                    if not last:
                        pN = psA()
                        pT = psA()
                        for hl in range(4):
                            kb.mm(pN[:, hl * 128:(hl + 1) * 128], NTcur[hl], Ncur[hl])
                            kb.mm(pT[:, hl * 128:(hl + 1) * 128], Ncur[hl], NTcur[hl])
                    if lev >= 2:
                        pGm = psA()
                        for hl in range(4):
                            kb.mm(pGm[:, hl * 128:(hl + 1) * 128], NTcur[hl], Gm[:, hl, :])
                        gmf = V(Gm.ap.rearrange("p h t -> p (h t)"), Gm.buf)
                        kb.tt("dve", gmf, pGm, gmf, ALU.add)
                    if not last:
                        kb.cp("act", V(Np[nxt].ap.rearrange("p h t -> p (h t)"), Np[nxt].buf), pN)
                        kb.cp("act", V(NTp[nxt].ap.rearrange("p h t -> p (h t)"), NTp[nxt].buf), pT)
                    cur = nxt
                    yield
                p = psA()
                for hl in range(4):
                    h = hs[hl]
                    kb.mm(p[:, hl * 128:hl * 128 + 64], Gm[:, hl, :], At_[:, h * 64:(h + 1) * 64])
                    kb.mm(p[:, hl * 128 + 64:hl * 128 + 128], Gm[:, hl, :], U1[:, hl, :])
                kb.cp("act", V(AhU.ap.rearrange("p h c -> p (h c)"), AhU.buf), p)
                yield
                pRh = psA()
                pP = psA()
                for hl in range(4):
                    h = hs[hl]
                    kb.mm(pRh[0:64, hl * 128:(hl + 1) * 128], AhU[:, hl, 0:64], S1[hl][:, 128:256], start=True, stop=False)
                    kb.mm(pRh[0:64, hl * 128:(hl + 1) * 128], Rt_[:, h * 64:(h + 1) * 64], ident_b, start=False, stop=True)
                    kb.mm(pP[0:64, hl * 64:(hl + 1) * 64], AhU[:, hl, 0:64], Bh_[:, h * 64:(h + 1) * 64])
                kb.cp("act", V(RhT.ap.rearrange("p h t -> p (h t)"), RhT.buf), pRh[0:64, :])
                kb.tt("dve", V(Pm.ap.rearrange("p h k -> p (h k)"), Pm.buf), pP[0:64, 0:256], eTot_[:, hg * 256:(hg + 1) * 256], ALU.add)
                yield
                pYl = psA()
                pZ = psA()
                for hl in range(4):
                    h = hs[hl]
                    kb.mm(pYl[:, hl * 64:(hl + 1) * 64], S1[hl][:, 128:256], AhU[:, hl, 64:128], start=True, stop=False)
                    kb.mm(pYl[:, hl * 64:(hl + 1) * 64], S1[hl][:, 384:512], Vb_[:, h * 64:(h + 1) * 64], start=False, stop=False)
                    kb.mm(pYl[:, hl * 64:(hl + 1) * 64], RhT[:, hl, :], STb[hg][:, hl, :], start=False, stop=True)
                    kb.mm(pZ[0:64, hl * 128:(hl + 1) * 128], MTb[hg][:, hl, :], RhT[:, hl, :])
                kb.cp("act", Yl[hg], pYl[:, 0:256])
                kb.cp("act", V(ZT[hg].ap.rearrange("p h t -> p (h t)"), ZT[hg].buf), pZ[0:64, :])
                kb.dma("sp", rec_Y[n][:, hg * 256:(hg + 1) * 256], Yl[hg])
                kb.dma("sp", rec_ZT[n][:, hg * 512:(hg + 1) * 512], V(ZT[hg].ap.rearrange("p h t -> p (h t)"), ZT[hg].buf))
                yield
                pS = psA()
                pM = psA()
                for hl in range(4):
                    h = hs[hl]
                    kb.mm(pS[0:64, hl * 64:(hl + 1) * 64], Bh_[:, h * 64:(h + 1) * 64], AhU[:, hl, 64:128], start=True, stop=False)
                    kb.mm(pS[0:64, hl * 64:(hl + 1) * 64], Kh_[:, h * 64:(h + 1) * 64], Vb_[:, h * 64:(h + 1) * 64], start=False, stop=False)
                    kb.mm(pS[0:64, hl * 64:(hl + 1) * 64], Pm[:, hl, :], ST[hg][:, hl, :], start=False, stop=True)
                    kb.mm(pM[0:64, hl * 64:(hl + 1) * 64], Pm[:, hl, :], MT[hg][:, hl, :])
                kb.cp("act", V(ST[hg].ap.rearrange("p h k -> p (h k)"), ST[hg].buf), pS[0:64, 0:256])
                kb.cp("act", V(STb[hg].ap.rearrange("p h k -> p (h k)"), STb[hg].buf), pS[0:64, 0:256])
                kb.cp("act", V(MT[hg].ap.rearrange("p h k -> p (h k)"), MT[hg].buf), pM[0:64, 0:256])
                kb.cp("act", V(MTb[hg].ap.rearrange("p h k -> p (h k)"), MTb[hg].buf), pM[0:64, 0:256])
                yield

            def drive(gens, weights):
                gens = list(gens)
                weights = list(weights)
                while gens:
                    dead = []
                    for idx in range(len(gens)):
                        for _ in range(weights[idx]):
                            try:
                                next(gens[idx])
                            except StopIteration:
                                dead.append(idx)
                                break
                    for idx in reversed(dead):
                        gens.pop(idx)
                        weights.pop(idx)

            drive([fsp_gen(0)], [1])
            for n in range(NT):
                gens = [rw_gen(n, 0), rw_gen(n, 1)]
                wts = [1, 1]
                if n + 1 < NT:
                    if FSP_FIRST:
                        gens.insert(0, fsp_gen(n + 1))
                        wts.insert(0, FSP_W)
                    else:
                        gens.append(fsp_gen(n + 1))
                        wts.append(FSP_W)
                drive(gens, wts)
            kb.barrier()
            if stop_after == "A":
                kb.finish()
                return nc, dbg_outs

        SUMM = kb.sb(G, "SUMM", [128, 1056], F32)
        kb.memset("pool", SUMM, 0.0)
        p = ps()
        for h in range(8):
            kb.tr(p[0:64, h * 64:(h + 1) * 64], MT[h // 4][:, h % 4, :], ident_f[0:64, 0:64])
        kb.cp("act", SUMM[0:64, 0:512], p[0:64, :])
        for hg_ in range(2):
            kb.cp("dve", SUMM[0:64, 512 + hg_ * 256:512 + (hg_ + 1) * 256], V(ST[hg_].ap.rearrange("p h k -> p (h k)"), ST[hg_].buf))
        kb.cp("dve", SUMM[:, 1024:1040], XIN_RE)
        kb.cp("dve", SUMM[:, 1040:1056], XIN_IM)
        kb.dma("sp", summ_d, SUMM)

        kb.memset("pool", STIN, 0.0)
        kb.memset("pool", XC_RE, 0.0)
        kb.memset("pool", XC_IM, 0.0)
        kb.memset("pool", gn_eps, GN_EPS)
        kb.dma("sp", FM, bcast(fmask_h, 8))
        LSR = kb.sb(G, "LSR", [128, 16], F32)
        LSI = kb.sb(G, "LSI", [128, 16], F32)
        L128R = kb.sb(G, "L128R", [128, 16], F32)
        L128I = kb.sb(G, "L128I", [128, 16], F32)
        XI_R = kb.sb(G, "XI_R", [128, 16], F32)
        XI_I = kb.sb(G, "XI_I", [128, 16], F32)
        XI_NI = kb.sb(G, "XI_NI", [128, 16], F32)
        cdummy = kb.sb(G, "cdummy", [128, 1], F32)
        B1 = ExitStack()
        WG = kb.sb(B1, "WG", [128, KC, 2048], BF16)
        WBR = kb.sb(B1, "WBR", [128, KC, D], BF16)
        WO = kb.sb(B1, "WO", [128, KC, D], BF16)
        WGL = kb.sb(B1, "WGL", [128, 4, 512], BF16)
        BGL = kb.sb(B1, "BGL", [128, 4], F32)
        lnxg_bc = kb.sb(B1, "lnxg_bc", [128, 512], F32)
        lnxb_bc = kb.sb(B1, "lnxb_bc", [128, 512], F32)
        kb.dma("pool", WG, V(w_in_d.ap[:, 2304:4352].rearrange("(kc p) c -> p kc c", p=128), w_in_d.buf))
        kb.dma("pool", WBR, V(w_br_d.ap.rearrange("(kc p) c -> p kc c", p=128), w_br_d.buf))
        kb.dma("pool", WO, V(w_out_d.ap.rearrange("(kc p) c -> p kc c", p=128), w_out_d.buf))
        kb.dma("pool", WGL, V(w_glu_d.ap.rearrange("(kc p) c -> p kc c", p=128), w_glu_d.buf))
        kb.dma("sp", BGL, V(bass.AP(tensor=b_glu_h, offset=0, ap=[[1, 128], [128, 4]]), b_glu_d.buf), allow_slow_non_contiguous=True)
        kb.dma("sp", lnxg_bc, bcast(lnx_g_h, 512))
        kb.dma("sp", lnxb_bc, bcast(lnx_b_h, 512))
        with B1:
            cc_sem_x = kb.allgather_issue(gath_d, summ_d, [[0, 1, 2, 3], [4, 5, 6, 7]]) if do_exchange else None
            if do_exchange:
                ER = kb.sb(B1, "ER", [128, 16, 128], BF16)
                EI = kb.sb(B1, "EI", [128, 16, 128], BF16)
                L1 = kb.sb(B1, "L1", [128, 16, 128], BF16)
                L2 = kb.sb(B1, "L2", [128, 16, 128], BF16)
                rp = kb.sb(B1, "rp", [128, 128], F32)
                xq = [kb.sb(B1, f"xq{i}", [128, 16], F32) for i in range(4)]
                kb.memset("pool", L1, 0.0)
                kb.memset("pool", L2, 0.0)
                for j in range(16):
                    kb.act(rp, TP1, AF.Exp, scale=LNA[:, j:j + 1])
                    kb.tt("dve", ER[:, j, :], rp, COS[:, j, :], ALU.mult)
                    kb.tt("pool", EI[:, j, :], rp, SIN[:, j, :], ALU.mult)

            xtB = [kb.sb(B1, f"xtB{i}", [128, D], F32) for i in range(2)]
            hbB = kb.sb(B1, "hbB", [128, D], BF16)
            ssB = kb.sb(B1, "ssB", [128, 1], F32)
            rstdB = kb.sb(B1, "rstdB", [128, 1], F32)
            hTB = kb.sb(B1, "hTB", [128, KC, 128], BF16)
            gate = [[kb.sb(B1, f"gate{i}_{par}", [128, D], BF16) for i in range(2)] for par in range(2)]
            yTl = kb.sb(B1, "yTl", [128, 512], F32)
            ga = kb.sb(B1, "ga", [128, 512], F32)
            gb_ = kb.sb(B1, "gb_", [128, 512], F32)
            zT = kb.sb(B1, "zT", [128, 4, 128], BF16)
            sg2 = kb.sb(B1, "sg2", [128, 4, 128], F32)
            oaT = [kb.sb(B1, f"oaT{i}", [128, 4, 128], BF16) for i in range(2)]
            YlB = kb.sb(B1, "YlB", [128, 512], F32)
            ZTl = kb.sb(B1, "ZTl", [64, 8, 128], BF16)
            gl_ = kb.sb(B1, "gl_", [128, 512], BF16)
            bvl = kb.sb(B1, "bvl", [128, 512], BF16)
            Yn = kb.sb(B1, "Yn", [128, 512], F32)
            sqB = kb.sb(B1, "sqB", [128, 512], F32)
            q8 = [kb.sb(B1, f"q8_{i}", [128, 8], F32) for i in range(4)]
            ob = kb.sb(B1, "ob", [128, 512], BF16)
            obT = [kb.sb(B1, f"obT{i}", [128, 4, 128], BF16) for i in range(2)]
            mg = kb.sb(B1, "mg", [128, D], F32)
            mg2 = kb.sb(B1, "mg2", [128, 512], F32)
            mgb = kb.sb(B1, "mgb", [128, D], BF16)
            mT = kb.sb(B1, "mT", [128, KC, 128], BF16)
            x1 = kb.sb(B1, "x1", [128, D], F32)
            if do_exchange:
                Ltmp = kb.sb(B1, "Ltmp", [128, 16, 128], BF16)
                XR_b = kb.sb(B1, "XR_b", [128, 16], BF16)

            def b1_front(n):
                par = n % 2
                xt_ = xtB[par]
                kb.dma("sp", xt_, x_d[n * 128:(n + 1) * 128, :])
                norm_tile(xt_, 128, g1_bc, hbB, None, ssB, rstdB)
                yield
                p = psA()
                pb = p.bitcast(BF16)
                for c in range(KC):
                    kb.tr(pb[:, c * 128:(c + 1) * 128], hbB[:, c * 128:(c + 1) * 128], ident_b)
                kb.cp("act", hTB, V(pb.ap.rearrange("p (c t) -> p c t", c=KC), p.buf))
                yield
                for half in range(2):
                    for cg in range(2):
                        p = psA()
                        c0 = half * 1024 + cg * 512
                        for kc in range(KC):
                            kb.mm(p, hTB[:, kc, :], WG[:, kc, c0:c0 + 512], start=(kc == 0), stop=(kc == KC - 1))
                        kb.act(gate[par][half][:, cg * 512:(cg + 1) * 512], p, AF.Sigmoid)
                        yield

            def b1_s5(n):
                par = n % 2
                kb.dma("sp", yTl, rec_yT[n])
                if do_exchange:
                    C3, N3 = CRE, NCIM
                    XR3 = V(XI_R.ap.unsqueeze(2).broadcast_to([128, 16, 128]), XI_R.buf)
                    XI3 = V(XI_I.ap.unsqueeze(2).broadcast_to([128, 16, 128]), XI_I.buf)
                    XN3 = V(XI_NI.ap.unsqueeze(2).broadcast_to([128, 16, 128]), XI_NI.buf)
                    kb.tt("dve", L1, C3, XR3, ALU.mult)
                    kb.tt("pool", Ltmp, N3, XI3, ALU.mult)
                    kb.tt("dve", L1, L1, Ltmp, ALU.add)
                    yield
                    kb.tt("pool", L2, N3, XR3, ALU.mult)
                    kb.tt("dve", Ltmp, C3, XN3, ALU.mult)
                    kb.tt("pool", L2, L2, Ltmp, ALU.add)
                    kb.tt("pool", xq[0], L128R, XI_R, ALU.mult)
                    kb.tt("pool", xq[1], L128I, XI_I, ALU.mult)
                    kb.tt("pool", xq[2], L128R, XI_I, ALU.mult)
                    kb.tt("pool", xq[3], L128I, XI_R, ALU.mult)
                    yield
                    kb.tt("pool", XI_R, xq[0], xq[1], ALU.subtract)
                    kb.tt("pool", XI_I, xq[2], xq[3], ALU.add)
                    kb.ts("pool", XI_NI, XI_I, -1.0, None, ALU.mult)
                    pc = psA()
                    for j in range(16):
                        cb, jj = divmod(j, 4)
                        kb.mm(pc[:, cb * 128:(cb + 1) * 128], L1[:, j, :], ER[:, j, :], start=(jj == 0), stop=False)
                        kb.mm(pc[:, cb * 128:(cb + 1) * 128], L2[:, j, :], EI[:, j, :], start=False, stop=(jj == 3))
                    kb.tt("dve", yTl, pc, yTl, ALU.add)
                    yield
                kb.tt("pool", ga, yTl, yTl, ALU.mult)
                kb.ts("pool", ga, ga, 0.044715, 1.0, ALU.mult, ALU.add)
                kb.tt("pool", ga, ga, yTl, ALU.mult)
                kb.act(gb_, ga, AF.Sigmoid, scale=1.5957691216057308)
                kb.tt("dve", V(zT.ap.rearrange("p c t -> p (c t)"), zT.buf), yTl, gb_, ALU.mult)
                yield
                p = psA()
                for ob_ in range(4):
                    for kc in range(4):
                        kb.mm(p[:, ob_ * 128:(ob_ + 1) * 128], WGL[:, kc, ob_ * 128:(ob_ + 1) * 128], zT[:, kc, :], start=(kc == 0), stop=(kc == 3))
                for ob_ in range(4):
                    kb.act(sg2[:, ob_, :], p[:, ob_ * 128:(ob_ + 1) * 128], AF.Sigmoid, bias=BGL[:, ob_:ob_ + 1])
                kb.tt("dve", oaT[par], zT, sg2, ALU.mult)
                if n == NT - 1:
                    dbg_out("oaT", V(oaT[par].ap.rearrange("p c t -> p (c t)"), oaT[par].buf), [128, 512], BF16)
                yield

            def b1_rw(n):
                par = n % 2
                kb.dma("sp", YlB, rec_Y[n])
                kb.dma("sp", V(ZTl.ap.rearrange("p h t -> p (h t)"), ZTl.buf), rec_ZT[n])
                kb.dma("sp", gl_, rec_g[n])
                kb.dma("sp", bvl, rec_bv[n])
                p = psA()
                for h in range(8):
                    kb.mm(p[:, h * 64:(h + 1) * 64], ZTl[:, h, :], STIN[:, h, :])
                kb.tt("dve", YlB, p, YlB, ALU.add)
                yield
                Y3 = V(YlB.ap.rearrange("p (h k) -> p h k", h=8), YlB.buf)
                kb.red(q8[0], Y3)
                kb.tt("pool", sqB, YlB, YlB, ALU.mult)
                kb.red(q8[1], V(sqB.ap.rearrange("p (h k) -> p h k", h=8), sqB.buf))
                kb.ts("dve", q8[0], q8[0], 1.0 / 64, None, ALU.mult)
                kb.tt("dve", q8[2], q8[0], q8[0], ALU.mult)
                kb.stt(q8[1], q8[1], 1.0 / 64, q8[2], ALU.mult, ALU.subtract)
                kb.act(q8[1], q8[1], AF.Sqrt, bias=gn_eps)
                kb.recip(q8[1], q8[1])
                yield
                Yn3 = V(Yn.ap.rearrange("p (h k) -> p h k", h=8), Yn.buf)
                kb.tt("dve", Yn3, Y3, V(q8[0].ap.unsqueeze(2).broadcast_to([128, 8, 64]), q8[0].buf), ALU.subtract)
                kb.tt("pool", Yn3, Yn3, V(q8[1].ap.unsqueeze(2).broadcast_to([128, 8, 64]), q8[1].buf), ALU.mult)
                yield
                kb.tt("pool", Yn, Yn, lnxg_bc, ALU.mult)
                kb.tt("pool", Yn, Yn, lnxb_bc, ALU.add)
                kb.tt("dve", Yn, Yn, bvl, ALU.add)
                kb.tt("dve", ob, Yn, gl_, ALU.mult)
                if n == NT - 1:
                    dbg_out("ob", ob, [128, 512], BF16)
                yield
                p = psA()
                pb = p.bitcast(BF16)
                for c in range(4):
                    kb.tr(pb[:, c * 128:(c + 1) * 128], ob[:, c * 128:(c + 1) * 128], ident_b)
                kb.cp("act", obT[par], V(pb.ap[:, 0:512].rearrange("p (c t) -> p c t", c=4), p.buf))
                yield

            def b1_join(n):
                par = n % 2
                for half in range(2):
                    src = oaT[par] if half == 0 else obT[par]
                    for cg in range(2):
                        p = psA()
                        for kc in range(4):
                            kb.mm(p, src[:, kc, :], WBR[:, half * 4 + kc, cg * 512:(cg + 1) * 512], start=(kc == 0), stop=(kc == 3))
                        if half == 0:
                            kb.tt("dve", mg[:, cg * 512:(cg + 1) * 512], p, gate[par][0][:, cg * 512:(cg + 1) * 512], ALU.mult)
                        else:
                            kb.tt("dve", mg2, p, gate[par][1][:, cg * 512:(cg + 1) * 512], ALU.mult)
                            kb.tt("pool", mgb[:, cg * 512:(cg + 1) * 512], mg[:, cg * 512:(cg + 1) * 512], mg2, ALU.add)
                        yield
                p = psA()
                pb = p.bitcast(BF16)
                for c in range(KC):
                    kb.tr(pb[:, c * 128:(c + 1) * 128], mgb[:, c * 128:(c + 1) * 128], ident_b)
                kb.cp("act", mT, V(pb.ap.rearrange("p (c t) -> p c t", c=KC), p.buf))
                yield
                for cg in range(2):
                    p = psA()
                    for kc in range(KC):
                        kb.mm(p, mT[:, kc, :], WO[:, kc, cg * 512:(cg + 1) * 512], start=(kc == 0), stop=(kc == KC - 1))
                    kb.tt("dve", x1[:, cg * 512:(cg + 1) * 512], p, xtB[par][:, cg * 512:(cg + 1) * 512], ALU.add)
                    yield
                kb.dma("sp", rec_x1[n], x1)
                if n == NT - 1:
                    dbg_out("x1", x1, [128, D])
                yield

            for _ in b1_front(0):
                pass
            if do_exchange:
                with ExitStack() as X:
                    GA = kb.sb(X, "GA", [128, 4, 1056], F32)
                    STf = kb.sb(X, "STf", [64, 8, 64], F32)
                    dfx = kb.sb(X, "dfx", [64, 512], F32)
                    e1 = [kb.sb(X, f"e1_{i}", [128, 16], F32) for i in range(4)]
                    kb.allgather_wait(cc_sem_x, gath_d, summ_d, cdummy)
                    kb.dma("sp", GA, V(gath_d.ap.rearrange("(r p) c -> p r c", p=128), gath_d.buf))
                    kb.cp("dve", LSR, ABRE)
                    kb.cp("dve", LSI, ABIM)
                    nsq = int(round(math.log2(NT * 128)))
                    assert 2 ** nsq == NT * 128
                    for i in range(nsq):
                        kb.tt("dve", e1[0], LSR, LSR, ALU.mult)
                        kb.tt("dve", e1[1], LSI, LSI, ALU.mult)
                        kb.tt("dve", e1[2], LSR, LSI, ALU.mult)
                        kb.tt("dve", LSR, e1[0], e1[1], ALU.subtract)
                        kb.ts("dve", LSI, e1[2], 2.0, None, ALU.mult)
                        if i == 6:
                            kb.cp("dve", L128R, LSR)
                            kb.cp("dve", L128I, LSI)
                    kb.memset("pool", STf, 0.0)
                    kb.memset("pool", XC_RE, 0.0)
                    kb.memset("pool", XC_IM, 0.0)
                    STf2 = V(STf.ap.rearrange("p h k -> p (h k)"), STf.buf)
                    for r in range(3):
                        p = ps()
                        for h in range(8):
                            kb.mm(p[0:64, h * 64:(h + 1) * 64], GA[0:64, r, h * 64:(h + 1) * 64], STf[:, h, :])
                        kb.tt("dve", dfx, p[0:64, :], GA[0:64, r, 512:1024], ALU.add)
                        kb.tt("dve", dfx, dfx, STf2, ALU.subtract)
                        kb.stt(STf2, dfx, FM[0:64, r:r + 1], STf2, ALU.mult, ALU.add)
                        kb.tt("dve", e1[0], LSR, XC_RE, ALU.mult)
                        kb.tt("dve", e1[1], LSI, XC_IM, ALU.mult)
                        kb.tt("dve", e1[0], e1[0], e1[1], ALU.subtract)
                        kb.tt("dve", e1[0], e1[0], GA[:, r, 1024:1040], ALU.add)
                        kb.tt("dve", e1[2], LSR, XC_IM, ALU.mult)
                        kb.tt("dve", e1[3], LSI, XC_RE, ALU.mult)
                        kb.tt("dve", e1[2], e1[2], e1[3], ALU.add)
                        kb.tt("dve", e1[2], e1[2], GA[:, r, 1040:1056], ALU.add)
                        kb.tt("dve", e1[0], e1[0], XC_RE, ALU.subtract)
                        kb.tt("dve", e1[2], e1[2], XC_IM, ALU.subtract)
                        kb.stt(XC_RE, e1[0], FM[:, r:r + 1], XC_RE, ALU.mult, ALU.add)
                        kb.stt(XC_IM, e1[2], FM[:, r:r + 1], XC_IM, ALU.mult, ALU.add)
                    kb.cp("dve", V(STIN.ap.rearrange("p h k -> p (h k)"), STIN.buf), STf2)
            kb.cp("dve", XI_R, XC_RE)
            kb.cp("dve", XI_I, XC_IM)
            kb.ts("dve", XI_NI, XC_IM, -1.0, None, ALU.mult)

            for n in range(NT + 1):
                if B1SEQ:
                    if n < NT:
                        for g_ in ((b1_front(n),) if n else ()) + (b1_s5(n), b1_rw(n), b1_join(n)):
                            for _ in g_:
                                pass
                    continue
                gens, wts = [], []
                if n >= 1:
                    gens.append(b1_join(n - 1))
                    wts.append(1)
                if n < NT:
                    if n >= 1:
                        gens.append(b1_front(n))
                        wts.append(1)
                    gens += [b1_s5(n), b1_rw(n)]
                    wts += [1, 1]
                drive(gens, wts)
            kb.barrier()
            if stop_after == "B1":
                kb.finish()
                return nc, dbg_outs

        with ExitStack() as B2:
            W1b = [kb.sb(B2, f"W1b{i}", [128, KC, 512], BF16) for i in range(8)]
            W2b = [kb.sb(B2, f"W2b{i}", [128, 4, D], BF16) for i in range(8)]
            g2_bc = kb.sb(B2, "g2_bc", [128, D], F32)
            gf_bc = kb.sb(B2, "gf_bc", [128, D], F32)
            for f4 in range(8):
                kb.dma("pool", W1b[f4], V(w_ff1_d.ap[:, f4 * 512:(f4 + 1) * 512].rearrange("(kc p) c -> p kc c", p=128), w_ff1_d.buf))
            for f4 in range(8):
                kb.dma("pool", W2b[f4], V(w_ff2_d.ap[f4 * 512:(f4 + 1) * 512, :].rearrange("(kc p) c -> p kc c", p=128), w_ff2_d.buf))
            kb.dma("sp", g2_bc, bcast(norm2_h, D))
            kb.dma("sp", gf_bc, bcast(normf_h, D))
            x1t = [kb.sb(B2, f"x1t{i}", [128, D], F32) for i in range(3)]
            hbC = kb.sb(B2, "hbC", [128, D], BF16)
            ssC = [kb.sb(B2, f"ssC{i}", [128, 1], F32) for i in range(2)]
            rstdC = [kb.sb(B2, f"rstdC{i}", [128, 1], F32) for i in range(2)]
            h2T = [kb.sb(B2, f"h2T{i}", [128, KC, 128], BF16) for i in range(2)]
            rl = [kb.sb(B2, f"rl{i}", [128, 512], BF16) for i in range(2)]
            f1T = [kb.sb(B2, f"f1T{i}", [128, 32, 128], BF16) for i in range(2)]
            junkC = kb.sb(B2, "junkC", [128, D], BF16)

            def b2_front(n):
                par = n % 2
                kb.dma("sp", x1t[n % 3], rec_x1[n])
                norm_tile(x1t[n % 3], 128, g2_bc, hbC, None, ssC[0], rstdC[0])
                yield
                p = psA()
                pb = p.bitcast(BF16)
                for c in range(KC):
                    kb.tr(pb[:, c * 128:(c + 1) * 128], hbC[:, c * 128:(c + 1) * 128], ident_b)
                kb.cp("act", h2T[par], V(pb.ap.rearrange("p (c t) -> p c t", c=KC), p.buf))
                yield

            def b2_ffn1(n):
                par = n % 2
                for f4 in range(8):
                    p = psA()
                    for i in range(4):
                        fb = f4 * 4 + i
                        for kc in range(KC):
                            kb.mm(p[:, i * 128:(i + 1) * 128], W1b[f4][:, kc, i * 128:(i + 1) * 128], h2T[par][:, kc, :], start=(kc == 0), stop=(kc == KC - 1))
                    r_ = rl[f4 % 2]
                    kb.act(r_, p, AF.Relu)
                    kb.tt("pool", V(f1T[par].ap[:, f4 * 4:(f4 + 1) * 4, :].rearrange("p c t -> p (c t)"), f1T[par].buf), r_, r_, ALU.mult)
                    yield

            def b2_ffn2(n):
                par = n % 2
                for cg in range(2):
                    p = psA()
                    for fb in range(32):
                        kb.mm(p, f1T[par][:, fb, :], W2b[fb // 4][:, fb % 4, cg * 512:(cg + 1) * 512], start=(fb == 0), stop=(fb == 31))
                    kb.tt("dve", x1t[n % 3][:, cg * 512:(cg + 1) * 512], p, x1t[n % 3][:, cg * 512:(cg + 1) * 512], ALU.add)
                    yield
                norm_tile(x1t[n % 3], 128, gf_bc, x1t[n % 3], junkC, ssC[1], rstdC[1])
                kb.dma("sp", out_d[n * 128:(n + 1) * 128, :], x1t[n % 3])
                yield

            for n in range(NT + 2):
                if B2SEQ:
                    if n < NT:
                        for g_ in (b2_front(n), b2_ffn1(n), b2_ffn2(n)):
                            for _ in g_:
                                pass
                    continue
                gens, wts = [], []
                if 0 <= n - 2 < NT:
                    gens.append(b2_ffn2(n - 2))
                    wts.append(1)
                if 0 <= n - 1 < NT:
                    gens.append(b2_ffn1(n - 1))
                    wts.append(3)
                if n < NT:
                    gens.append(b2_front(n))
                    wts.append(1)
                drive(gens, wts)

        kb.finish()
    return nc, dbg_outs


_W_KEYS = ["norm1_g", "w_in", "lam_re", "lam_im", "log_dt", "b_re", "b_im", "c_re", "c_im", "d_skip", "w_glu", "b_glu",
           "mu_rwkv", "w0", "w2", "a0", "a2", "g2", "k_k", "k_a", "r_k", "lnx_g", "lnx_b", "w_branch", "w_out",
           "norm2_g", "w_ff1", "w_ff2"]


def _prep_weights(inputs):
    w = {}
    for k in _W_KEYS:
        a = np.asarray(inputs[k], dtype=np.float32)[0]
        if k == "r_k":
            a = a.reshape(D_RWKV)
        w[k] = np.ascontiguousarray(a)
    w["norm_f_g"] = np.ascontiguousarray(np.asarray(inputs["norm_f_g"], dtype=np.float32))
    return w


def make_in_maps(inputs, NT, n_cores=8):
    x = np.asarray(inputs["x"], dtype=np.float32)
    w = _prep_weights(inputs)
    seg = NT * TOK
    maps = []
    for c in range(n_cores):
        b, j = divmod(c, NSEG)
        m = dict(w)
        m["x"] = np.ascontiguousarray(x[b, j * seg:(j + 1) * seg])
        if j > 0:
            m["xprev"] = np.ascontiguousarray(x[b, j * seg - 1:j * seg])
        else:
            m["xprev"] = np.zeros((1, D), np.float32)
        m["fmask"] = np.array([1.0 if r < j else 0.0 for r in range(8)], np.float32)
        maps.append(m)
    return maps


def kernel(**inputs):
    NT = NT_FULL
    nc, _ = build(NT)
    maps = make_in_maps(inputs, NT)
    res = run_bass_kernel_spmd(nc, maps, core_ids=list(range(8)))
    out = np.zeros((2, SEQ, D), np.float32)
    seg = NT * TOK
    for c in range(8):
        b, j = divmod(c, NSEG)
        out[b, j * seg:(j + 1) * seg] = res.results[c]["out"]
    return out
```

```python
import math
from contextlib import ExitStack

import numpy as np
import concourse.bass as bass
import concourse.mybir as mybir
from concourse.bass_utils import run_bass_kernel_spmd

F32 = mybir.dt.float32
BF16 = mybir.dt.bfloat16
I32 = mybir.dt.int32
AF = mybir.ActivationFunctionType
ALU = mybir.AluOpType
AX = mybir.AxisListType

D = 1024
KC = 8
TOK = 128
SEQ = 8192
NSEG = 4
NT_FULL = SEQ // NSEG // TOK
D_SSM = 512
D_RWKV = 512
NH = 8
HD = 64
D_RIN = 1792
D_IN = 4352
D_FF = 4096
RMS_EPS = 1e-6
GN_EPS = 64e-5
NEG_EH = -math.exp(-0.5)
TWO_PI = 2.0 * math.pi
SAME_ENGINE_WAITS = True
import os
SEQ_MODE = int(os.environ.get('SEQ_MODE', '0'))
TOG = os.environ.get('TOG', '')
RW_STOP = int(os.environ.get('RW_STOP', '99'))
FSP_W = int(os.environ.get('FSP_W', '3'))
FSP_FIRST = int(os.environ.get('FSP_FIRST', '1'))
INLINE_WAITS = int(os.environ.get('INLINE_WAITS', '1'))
B1SEQ = int(os.environ.get('B1SEQ', '0'))
B2SEQ = int(os.environ.get('B2SEQ', '0'))


class Buf:
    __slots__ = ("name", "w", "r", "excl")

    def __init__(self, name, excl=False):
        self.name = name
        self.w = None
        self.r = {}
        self.excl = excl


class V:
    __slots__ = ("ap", "buf")

    def __init__(self, ap, buf):
        self.ap = ap
        self.buf = buf

    def __getitem__(self, idx):
        return V(self.ap[idx], self.buf)

    def bitcast(self, dt):
        return V(self.ap.bitcast(dt), self.buf)


class KB:
    NDS = 8

    def __init__(self, nc):
        self.nc = nc
        self.eng = {"pe": nc.tensor, "act": nc.scalar, "dve": nc.vector, "pool": nc.gpsimd, "sp": nc.sync}
        self.sem = {e: nc.alloc_semaphore("sem_" + e) for e in ["pe", "act", "dve", "pool"]}
        self.semobj = dict(self.sem)
        self.cnt = {e: 0 for e in self.sem}
        self.waited = {e: {} for e in self.eng}
        self.pending = {e: {} for e in self.eng}
        self.dq = {}
        self.dma_last = {}
        self.nops = 0

    def sb(self, stack, name, shape, dt):
        h = stack.enter_context(self.nc.sbuf_tensor(name, list(shape), dt))
        return V(h[:], Buf(name))

    def dram(self, name, shape, dt, kind):
        h = self.nc.dram_tensor(name, list(shape), dt, kind=kind)
        return V(h.ap(), Buf(name)), h

    def _deps(self, reads, writes):
        need = {}
        for v in reads:
            w = v.buf.w
            if w is not None:
                need[w[0]] = max(need.get(w[0], 0), w[1])
        for v in writes:
            w = v.buf.w
            if w is not None:
                need[w[0]] = max(need.get(w[0], 0), w[1])
            for k, val in v.buf.r.items():
                need[k] = max(need.get(k, 0), val)
        return need

    def _wait(self, E, need, defer_one=False):
        for k, val in self.pending[E].items():
            need[k] = max(need.get(k, 0), val)
        self.pending[E] = {}
        todo = []
        for k, val in need.items():
            if k == E and (E == "pe" or not SAME_ENGINE_WAITS):
                continue
            if self.waited[E].get(k, 0) >= val:
                continue
            todo.append((k, val))
            self.waited[E][k] = val
        inline = None
        if defer_one and todo and INLINE_WAITS:
            inline = todo.pop()
        for k, val in todo:
            self.eng[E].wait_ge(self.semobj[k], val)
        return inline

    def _mark(self, tok, reads, writes):
        for v in writes:
            v.buf.w = tok
            v.buf.r = {}
        for v in reads:
            r = v.buf.r
            r[tok[0]] = max(r.get(tok[0], 0), tok[1])

    def op(self, E, fn, reads, writes, inline_ok=True):
        if E != "pe":
            ex = [v for v in reads if v.buf.excl]
            if ex:
                writes = list(writes) + ex
        need = self._deps(reads, writes)
        inline = self._wait(E, need, defer_one=(inline_ok and E != "pe"))
        ins = fn(self.eng[E])
        if inline is not None:
            ins._wait_ge(self.semobj[inline[0]], inline[1])
        self.cnt[E] += 1
        ins.then_inc(self.sem[E], 1)
        self._mark((E, self.cnt[E]), reads, writes)
        self.nops += 1
        return ins

    def dma(self, Q, out, in_, **kw):
        need = self._deps([in_], [out])
        st = self.dq.get(Q)
        if st is None:
            st = {"sems": [self.nc.alloc_semaphore(f"dsem_{Q}_{i}") for i in range(self.NDS)],
                  "uses": [0] * self.NDS, "next": 0}
            self.dq[Q] = st
            for i, sm in enumerate(st["sems"]):
                self.semobj[("d", Q, i)] = sm
        k = st["next"]
        st["next"] = (k + 1) % self.NDS
        key = ("d", Q, k)
        if st["uses"][k] > 0:
            need[key] = max(need.get(key, 0), 16 * st["uses"][k])
        self._wait(Q, need)
        self.eng[Q].dma_start(out=out.ap, in_=in_.ap, **kw).then_inc(st["sems"][k], 16)
        st["uses"][k] += 1
        tok = (key, 16 * st["uses"][k])
        self.dma_last[key] = tok[1]
        self._mark(tok, [in_], [out])
        self.nops += 1

    def allgather_issue(self, out, in_, groups):
        need = self._deps([in_], [out])
        self._wait("pool", need)
        cc_sem = self.nc.alloc_semaphore("cc_sem")
        self.nc.gpsimd.collective_compute("AllGather", ALU.bypass, replica_groups=groups, ins=[in_.ap], outs=[out.ap]).then_inc(cc_sem)
        return cc_sem

    def allgather_wait(self, cc_sem, out, in_, dummy):
        self.nc.gpsimd.wait_ge(cc_sem, 1)
        self.op("pool", lambda e: e.memset(dummy.ap, 0.0), [in_], [dummy, out])

    def barrier(self):
        allt = {}
        for e, c in self.cnt.items():
            if c > 0:
                allt[e] = c
        for k, val in self.dma_last.items():
            allt[k] = val
        for E in self.eng:
            for k, val in allt.items():
                self.pending[E][k] = max(self.pending[E].get(k, 0), val)

    def finish(self):
        need = {}
        for k, val in self.dma_last.items():
            need[k] = val
        for e, c in self.cnt.items():
            if c > 0:
                need[e] = c
        self._wait("sp", need)

    def mm(self, out, lhsT, rhs, start=True, stop=True):
        self.op("pe", lambda e: e.matmul(out.ap, lhsT.ap, rhs.ap, start=start, stop=stop), [lhsT, rhs], [out])

    def tr(self, out, in_, ident):
        self.op("pe", lambda e: e.transpose(out.ap, in_.ap, ident.ap), [in_, ident], [out])

    def act(self, out, in_, func, bias=0.0, scale=1.0, accum=None):
        reads = [in_]
        b = bias
        sc = scale
        if isinstance(bias, V):
            reads.append(bias)
            b = bias.ap
        if isinstance(scale, V):
            reads.append(scale)
            sc = scale.ap
        writes = [out]
        kw = {}
        if accum is not None:
            writes.append(accum)
            kw["accum_out"] = accum.ap
        self.op("act", lambda e: e.activation(out=out.ap, in_=in_.ap, func=func, bias=b, scale=sc, **kw), reads, writes, inline_ok=(accum is None))

    def ts(self, E, out, in0, s1, s2, op0, op1=None):
        reads = [in0]
        a1, a2 = s1, s2
        if isinstance(s1, V):
            reads.append(s1)
            a1 = s1.ap
        if isinstance(s2, V):
            reads.append(s2)
            a2 = s2.ap
        if op1 is None:
            self.op(E, lambda e: e.tensor_scalar(out=out.ap, in0=in0.ap, scalar1=a1, scalar2=None, op0=op0), reads, [out])
        else:
            self.op(E, lambda e: e.tensor_scalar(out=out.ap, in0=in0.ap, scalar1=a1, scalar2=a2, op0=op0, op1=op1), reads, [out])

    def tt(self, E, out, in0, in1, op):
        self.op(E, lambda e: e.tensor_tensor(out=out.ap, in0=in0.ap, in1=in1.ap, op=op), [in0, in1], [out])

    def stt(self, out, in0, scalar, in1, op0, op1):
        reads = [in0, in1]
        sc = scalar
        if isinstance(scalar, V):
            reads.append(scalar)
            sc = scalar.ap
        self.op("dve", lambda e: e.scalar_tensor_tensor(out=out.ap, in0=in0.ap, scalar=sc, in1=in1.ap, op0=op0, op1=op1), reads, [out])

    def cp(self, E, out, in_):
        if E == "act":
            self.op("act", lambda e: e.activation(out=out.ap, in_=in_.ap, func=AF.Copy), [in_], [out])
        else:
            self.op(E, lambda e: e.tensor_copy(out=out.ap, in_=in_.ap), [in_], [out])

    def red(self, out, in_, op=ALU.add):
        self.op("dve", lambda e: e.tensor_reduce(out=out.ap, in_=in_.ap, axis=AX.X, op=op), [in_], [out])

    def scan(self, out, d0, d1, init, op0=ALU.mult, op1=ALU.add):
        reads = [d0, d1]
        ini = init
        if isinstance(init, V):
            reads.append(init)
            ini = init.ap
        self.op("dve", lambda e: e.tensor_tensor_scan(out=out.ap, data0=d0.ap, data1=d1.ap, initial=ini, op0=op0, op1=op1), reads, [out])

    def recip(self, out, in_):
        self.op("dve", lambda e: e.reciprocal(out=out.ap, in_=in_.ap), [in_], [out])

    def memset(self, E, out, val):
        self.op(E, lambda e: e.memset(out.ap, val), [], [out])

    def asel(self, out, in_, pattern, cmp, fill, base, cm):
        self.op("pool", lambda e: e.affine_select(out=out.ap, in_=in_.ap, pattern=pattern, compare_op=cmp, fill=fill,
                                                  base=base, channel_multiplier=cm), [in_], [out])


def build(NT, dbg=None, do_exchange=True, n_cores=8, stop_after=None):
    nc = bass.Bass("TRN2", target_bir_lowering=False)
    kb = KB(nc)
    dbg_outs = {}

    def din(name, shape):
        v, h = kb.dram(name, shape, F32, "ExternalInput")
        return v, h

    x_d, _ = din("x", [NT * TOK, D])
    xprev_d, _ = din("xprev", [1, D])
    fmask_d, fmask_h = din("fmask", [8])
    norm1_d, norm1_h = din("norm1_g", [D])
    w_in_d, _ = din("w_in", [D, D_IN])
    lam_re_d, lam_re_h = din("lam_re", [32, 64])
    lam_im_d, lam_im_h = din("lam_im", [32, 64])
    log_dt_d, log_dt_h = din("log_dt", [32])
    b_re_d, _ = din("b_re", [32, 64, 16])
    b_im_d, _ = din("b_im", [32, 64, 16])
    c_re_d, _ = din("c_re", [32, 16, 64])
    c_im_d, _ = din("c_im", [32, 16, 64])
    d_skip_d, d_skip_h = din("d_skip", [D_SSM])
    w_glu_d, _ = din("w_glu", [D_SSM, D_SSM])
    b_glu_d, b_glu_h = din("b_glu", [D_SSM])
    mu_d, mu_h = din("mu_rwkv", [D_RIN])
    w0_d, w0_h = din("w0", [D_RWKV])
    w2_d, _ = din("w2", [64, D_RWKV])
    a0_d, a0_h = din("a0", [D_RWKV])
    a2_d, _ = din("a2", [64, D_RWKV])
    g2_d, _ = din("g2", [128, D_RWKV])
    k_k_d, k_k_h = din("k_k", [D_RWKV])
    k_a_d, k_a_h = din("k_a", [D_RWKV])
    r_k_d, r_k_h = din("r_k", [D_RWKV])
    lnx_g_d, lnx_g_h = din("lnx_g", [D_RWKV])
    lnx_b_d, lnx_b_h = din("lnx_b", [D_RWKV])
    w_br_d, _ = din("w_branch", [D, D])
    w_out_d, _ = din("w_out", [D, D])
    norm2_d, norm2_h = din("norm2_g", [D])
    w_ff1_d, _ = din("w_ff1", [D, D_FF])
    w_ff2_d, _ = din("w_ff2", [D_FF, D])
    normf_d, normf_h = din("norm_f_g", [D])
    out_d, _ = kb.dram("out", [NT * TOK, D], F32, "ExternalOutput")

    rec_yT, _ = kb.dram("rec_yT", [NT, 128, 512], F32, "Internal")
    rec_Y, _ = kb.dram("rec_Y", [NT, 128, 512], F32, "Internal")
    rec_ZT, _ = kb.dram("rec_ZT", [NT, 64, 1024], BF16, "Internal")
    rec_g, _ = kb.dram("rec_g", [NT, 128, 512], BF16, "Internal")
    rec_bv, _ = kb.dram("rec_bv", [NT, 128, 512], BF16, "Internal")
    rec_x1, _ = kb.dram("rec_x1", [NT, 128, D], F32, "Internal")
    summ_d, _ = kb.dram("summ_d", [128, 1056], F32, "Internal")
    gath_d, _ = kb.dram("gath_d", [4 * 128, 1056], F32, "Internal")

    def bcast(h, n, off=0):
        return V(bass.AP(tensor=h, offset=off, ap=[[0, 128], [1, n]]), Buf("bc"))

    def dbg_out(name, v, shape, dt=F32):
        if dbg is None or name not in dbg:
            return
        o, _ = kb.dram("dbg_" + name, shape, dt, "ExternalOutput")
        kb.dma("sp", o, v)
        dbg_outs[name] = shape

    root = ExitStack()
    with root:
        psb = []
        for i in range(8):
            h = root.enter_context(nc.psum_tensor(f"psb{i}", [128, 512], F32))
            psb.append(V(h[:], Buf(f"psb{i}", excl=True)))
        ps_rr = [0]

        def ps():
            b = psb[1 + ps_rr[0]]
            ps_rr[0] = (ps_rr[0] + 1) % 6
            return b

        ps_long = psb[7]

        G = root
        ones_f = kb.sb(G, "ones_f", [128, 128], F32)
        ident_f = kb.sb(G, "ident_f", [128, 128], F32)
        ident_b = kb.sb(G, "ident_b", [128, 128], BF16)
        kb.memset("pool", ones_f, 1.0)
        kb.asel(ident_f, ones_f, [[-1, 128]], ALU.is_equal, 0.0, 0, 1)
        kb.cp("pool", ident_b, ident_f)
        g1_bc = kb.sb(G, "g1_bc", [128, D], F32)
        kb.dma("sp", g1_bc, bcast(norm1_h, D))
        eps_c = kb.sb(G, "eps_c", [128, 1], F32)
        kb.memset("pool", eps_c, RMS_EPS)

        def norm_tile(xt, rows, gbc, hb, scr, ss, rstd):
            junk = scr if scr is not None else hb
            kb.act(junk[0:rows, :], xt[0:rows, :], AF.Square, accum=ss[0:rows, :])
            kb.act(ss[0:rows, :], ss[0:rows, :], AF.Sqrt, bias=eps_c[0:rows, :], scale=1.0 / D)
            kb.recip(rstd[0:rows, :], ss[0:rows, :])
            kb.stt(hb[0:rows, :], xt[0:rows, :], rstd[0:rows, :], gbc[0:rows, :], ALU.mult, ALU.mult)

        def transpose_rows(dstT, src_b, rows, nchunk):
            for c0 in range(0, nchunk, 8):
                p = ps()
                pb = p.bitcast(BF16)
                n = min(8, nchunk - c0)
                if rows == 128:
                    for c in range(n):
                        kb.tr(pb[:, c * 128:(c + 1) * 128], src_b[:, (c0 + c) * 128:(c0 + c + 1) * 128], ident_b)
                    kb.cp("act", dstT[:, c0:c0 + n, :], V(pb.ap[:, 0:n * 128].rearrange("p (c t) -> p c t", c=n), p.buf))
                else:
                    assert rows == 1
                    for c in range(n):
                        kb.tr(pb[:, 2 * c:2 * c + 1], src_b[0:1, (c0 + c) * 128:(c0 + c + 1) * 128], ident_b[0:1, 0:1])
                    kb.cp("act", dstT[:, c0:c0 + n, :], V(pb.ap[:, 0:2 * n].rearrange("p (c t) -> p c t", c=n)[:, :, 0:1], p.buf))

        COS = kb.sb(G, "COS", [128, 16, 128], BF16)
        SIN = kb.sb(G, "SIN", [128, 16, 128], BF16)
        FM = kb.sb(G, "FM", [128, 8], F32)
        CRE = kb.sb(G, "CRE", [128, 16, 128], BF16)
        NCIM = kb.sb(G, "NCIM", [128, 16, 128], BF16)
        RHO = kb.sb(G, "RHO", [128, 16], F32)
        PHI = kb.sb(G, "PHI", [128, 16], F32)
        LNA = kb.sb(G, "LNA", [128, 16], F32)
        ABRE = kb.sb(G, "ABRE", [128, 16], F32)
        ABIM = kb.sb(G, "ABIM", [128, 16], F32)
        R128C = kb.sb(G, "R128C", [128, 16], F32)
        R128S = kb.sb(G, "R128S", [128, 16], F32)
        DSK = kb.sb(G, "DSK", [128, 4], F32)
        TP1 = kb.sb(G, "TP1", [128, 128], F32)
        XIN_RE = kb.sb(G, "XIN_RE", [128, 16], F32)
        XIN_IM = kb.sb(G, "XIN_IM", [128, 16], F32)
        ST = [kb.sb(G, f"ST{i}", [64, 4, 64], F32) for i in range(2)]
        STb = [kb.sb(G, f"STb{i}", [64, 4, 64], BF16) for i in range(2)]
        MT = [kb.sb(G, f"MT{i}", [64, 4, 64], F32) for i in range(2)]
        STIN = kb.sb(G, "STIN", [64, 8, 64], BF16)
        XC_RE = kb.sb(G, "XC_RE", [128, 16], F32)
        XC_IM = kb.sb(G, "XC_IM", [128, 16], F32)
        gn_eps = kb.sb(G, "gn_eps", [128, 1], F32)

        with ExitStack() as A:
            WIN = kb.sb(A, "WIN", [128, KC, 2304], BF16)
            WSH = kb.sb(A, "WSH", [128, KC, D_RIN], BF16)
            BRE = kb.sb(A, "BRE", [128, 16, 128], BF16)
            BIM = kb.sb(A, "BIM", [128, 16, 128], BF16)
            WA2 = kb.sb(A, "WA2", [128, D_RWKV], BF16)
            G2B = kb.sb(A, "G2B", [128, D_RWKV], BF16)
            w0_bc = kb.sb(A, "w0_bc", [128, 512], F32)
            a0_bc = kb.sb(A, "a0_bc", [128, 512], F32)
            kk_bc = kb.sb(A, "kk_bc", [128, 512], F32)
            ka_bc = kb.sb(A, "ka_bc", [128, 512], F32)
            rk_bc = kb.sb(A, "rk_bc", [128, 512], F32)
            MASK4 = kb.sb(A, "MASK4", [128, 512], BF16)
            ML = kb.sb(A, "ML", [128, 128], BF16)
            TRI = kb.sb(A, "TRI", [128, 4, 128], F32)
            I64R = kb.sb(A, "I64R", [64, 64], BF16)

            with ExitStack() as P0:
                negc = kb.sb(P0, "negc", [128, 128], F32)
                kb.memset("pool", negc, NEG_EH)
                kb.asel(MASK4[:, 0:128], ones_f, [[1, 128]], ALU.is_gt, 0.0, 0, -1)
                kb.asel(MASK4[:, 128:256], ones_f, [[1, 128]], ALU.is_ge, 0.0, 0, -1)
                kb.cp("pool", MASK4[:, 256:512], MASK4[:, 0:256])
                kb.asel(ML, ones_f, [[-1, 128]], ALU.is_gt, 0.0, 0, 1)
                kb.asel(TRI[:, 0, :], negc, [[1, 128]], ALU.is_ge, 0.0, 0, -1)
                kb.asel(TRI[:, 1, :], negc, [[1, 128]], ALU.is_gt, 0.0, 0, -1)
                kb.asel(TRI[:, 2, :], negc, [[-1, 128]], ALU.is_gt, 0.0, 0, 1)
                kb.cp("pool", TRI[:, 3, :], negc)
                kb.cp("pool", I64R, ident_f[0:64, 0:64])
                kb.dma("pool", WA2[0:64, :], w2_d)
                kb.dma("pool", WA2[64:128, :], a2_d)
                kb.dma("pool", G2B, g2_d)
                for t_, h_ in ((w0_bc, w0_h), (a0_bc, a0_h), (kk_bc, k_k_h), (ka_bc, k_a_h), (rk_bc, r_k_h)):
                    kb.dma("sp", t_, bcast(h_, 512))
                kb.dma("pool", WIN[:, :, 0:512], V(w_in_d.ap[:, 0:512].rearrange("(kc p) c -> p kc c", p=128), w_in_d.buf))
                mu_bc = kb.sb(P0, "mu_bc", [128, D_RIN], F32)
                kb.dma("sp", mu_bc, bcast(mu_h, D_RIN))
                omm_bc = kb.sb(P0, "omm_bc", [128, D_RIN], F32)
                kb.ts("dve", omm_bc, mu_bc, -1.0, 1.0, ALU.mult, ALU.add)
                stg = [kb.sb(P0, f"stg{i}", [128, D_RIN], F32) for i in range(2)]
                for kc in range(KC):
                    s_ = stg[kc % 2]
                    kb.dma("sp", s_, w_in_d[kc * 128:(kc + 1) * 128, 512:2304])
                    kb.tt("dve", WSH[:, kc, :], s_, mu_bc, ALU.mult)
                    kb.tt("pool", WIN[:, kc, 512:2304], s_, omm_bc, ALU.mult)

                LR = kb.sb(P0, "LR", [128, 16], F32)
                LI = kb.sb(P0, "LI", [128, 16], F32)
                DT = kb.sb(P0, "DT", [128, 16], F32)
                kb.dma("sp", LR, V(bass.AP(tensor=lam_re_h, offset=0, ap=[[1, 128], [128, 16]]), lam_re_d.buf), allow_slow_non_contiguous=True)
                kb.dma("sp", LI, V(bass.AP(tensor=lam_im_h, offset=0, ap=[[1, 128], [128, 16]]), lam_im_d.buf), allow_slow_non_contiguous=True)
                for gl in range(2):
                    kb.dma("sp", DT[gl * 64:(gl + 1) * 64, :], V(bass.AP(tensor=log_dt_h, offset=gl, ap=[[0, 64], [2, 16]]), log_dt_d.buf),
                           allow_slow_non_contiguous=True)
                kb.dma("sp", DSK, V(bass.AP(tensor=d_skip_h, offset=0, ap=[[1, 128], [128, 4]]), d_skip_d.buf), allow_slow_non_contiguous=True)
                kb.act(DT, DT, AF.Exp)
                kb.ts("dve", LR, LR, -1e-4, None, ALU.min)
                kb.tt("dve", LNA, LR, DT, ALU.mult)
                kb.act(RHO, LNA, AF.Exp)
                TH = kb.sb(P0, "TH", [128, 16], F32)
                kb.tt("dve", TH, LI, DT, ALU.mult)
                sI = kb.sb(P0, "sI", [128, 128], I32)
                sF = kb.sb(P0, "sF", [128, 128], F32)
                sA = kb.sb(P0, "sA", [128, 128], F32)
                sB = kb.sb(P0, "sB", [128, 128], F32)

                def reduce_angle(dst, src, n, shift):
                    kb.ts("dve", sF[:, 0:n], src, 1.0 / TWO_PI, None, ALU.mult)
                    kb.cp("pool", sI[:, 0:n], sF[:, 0:n])
                    kb.cp("pool", sF[:, 0:n], sI[:, 0:n])
                    kb.stt(dst, sF[:, 0:n], -TWO_PI, src, ALU.mult, ALU.add)
                    if shift != 0.0:
                        kb.ts("dve", dst, dst, shift, None, ALU.add)
                    for _ in range(2):
                        kb.ts("dve", sF[:, 0:n], dst, -math.pi, TWO_PI, ALU.is_lt, ALU.mult)
                        kb.tt("dve", dst, dst, sF[:, 0:n], ALU.add)
                        kb.ts("dve", sF[:, 0:n], dst, math.pi, -TWO_PI, ALU.is_gt, ALU.mult)
                        kb.tt("dve", dst, dst, sF[:, 0:n], ALU.add)

                reduce_angle(PHI, TH, 16, 0.0)
                CP = kb.sb(P0, "CP", [128, 16], F32)
                SP_ = kb.sb(P0, "SP_", [128, 16], F32)
                kb.act(SP_, PHI, AF.Sin)
                reduce_angle(sA[:, 0:16], PHI, 16, math.pi / 2)
                kb.act(CP, sA[:, 0:16], AF.Sin)
                kb.tt("dve", ABRE, RHO, CP, ALU.mult)
                kb.tt("dve", ABIM, RHO, SP_, ALU.mult)
                DEN = kb.sb(P0, "DEN", [128, 16], F32)
                T1 = kb.sb(P0, "T1", [128, 16], F32)
                T2 = kb.sb(P0, "T2", [128, 16], F32)
                XM1 = kb.sb(P0, "XM1", [128, 16], F32)
                QRE = kb.sb(P0, "QRE", [128, 16], F32)
                QIM = kb.sb(P0, "QIM", [128, 16], F32)
                NQIM = kb.sb(P0, "NQIM", [128, 16], F32)
                kb.tt("dve", DEN, LR, LR, ALU.mult)
                kb.tt("dve", T1, LI, LI, ALU.mult)
                kb.tt("dve", DEN, DEN, T1, ALU.add)
                kb.recip(DEN, DEN)
                kb.ts("dve", XM1, ABRE, -1.0, None, ALU.add)
                kb.tt("dve", T1, XM1, LR, ALU.mult)
                kb.tt("dve", T2, ABIM, LI, ALU.mult)
                kb.tt("dve", T1, T1, T2, ALU.add)
                kb.tt("dve", QRE, T1, DEN, ALU.mult)
                kb.tt("dve", T1, ABIM, LR, ALU.mult)
                kb.tt("dve", T2, XM1, LI, ALU.mult)
                kb.tt("dve", T1, T1, T2, ALU.subtract)
                kb.tt("dve", QIM, T1, DEN, ALU.mult)
                kb.ts("dve", NQIM, QIM, -1.0, None, ALU.mult)
                kb.op("pool", lambda e: e.iota(sI.ap, pattern=[[1, 128]], base=1, channel_multiplier=0), [], [sI])
                kb.cp("pool", TP1, sI)
                kb.ts("dve", T1, PHI, 128.0, None, ALU.mult)
                reduce_angle(T2, T1, 16, 0.0)
                kb.act(R128S, T2, AF.Sin)
                reduce_angle(T2, T1, 16, math.pi / 2)
                kb.act(R128C, T2, AF.Sin)
                bA = kb.sb(P0, "bA", [128, 2048], F32)
                bB = kb.sb(P0, "bB", [128, 2048], F32)
                bF = kb.sb(P0, "bF", [128, 2048], F32)
                bI = kb.sb(P0, "bI", [128, 2048], I32)
                kb.tt("dve", V(bA.ap.rearrange("p (j t) -> p j t", j=16), bA.buf),
                      V(PHI.ap.unsqueeze(2).broadcast_to([128, 16, 128]), PHI.buf),
                      V(TP1.ap.unsqueeze(1).broadcast_to([128, 16, 128]), TP1.buf), ALU.mult)

                def reduce_angle_big(dst, src, shift):
                    kb.ts("dve", bF, src, 1.0 / TWO_PI, None, ALU.mult)
                    kb.cp("pool", bI, bF)
                    kb.cp("pool", bF, bI)
                    kb.stt(dst, bF, -TWO_PI, src, ALU.mult, ALU.add)
                    if shift != 0.0:
                        kb.ts("dve", dst, dst, shift, None, ALU.add)
                    for _ in range(2):
                        kb.ts("dve", bF, dst, -math.pi, TWO_PI, ALU.is_lt, ALU.mult)
                        kb.tt("dve", dst, dst, bF, ALU.add)
                        kb.ts("dve", bF, dst, math.pi, -TWO_PI, ALU.is_gt, ALU.mult)
                        kb.tt("dve", dst, dst, bF, ALU.add)

                reduce_angle_big(bB, bA, 0.0)
                kb.act(V(SIN.ap.rearrange("p j t -> p (j t)"), SIN.buf), bB, AF.Sin)
                reduce_angle_big(bB, bA, math.pi / 2)
                kb.act(V(COS.ap.rearrange("p j t -> p (j t)"), COS.buf), bB, AF.Sin)
                BXR = kb.sb(P0, "BXR", [128, 16, 128], F32)
                BXI = kb.sb(P0, "BXI", [128, 16, 128], F32)
                kb.memset("pool", BXR, 0.0)
                kb.memset("pool", BXI, 0.0)
                for j in range(16):
                    for gl in range(2):
                        c0 = (j % 4) * 32 + gl * 16
                        g = 2 * j + gl
                        kb.dma("sp", BXR[gl * 64:(gl + 1) * 64, j, c0:c0 + 16], b_re_d[g, :, :])
                        kb.dma("sp", BXI[gl * 64:(gl + 1) * 64, j, c0:c0 + 16], b_im_d[g, :, :])
                bt = [kb.sb(P0, f"bt{i}", [128, 128], BF16) for i in range(2)]
                for j in range(16):
                    kb.ts("dve", sA, BXI[:, j, :], QIM[:, j:j + 1], None, ALU.mult)
                    kb.stt(bt[0], BXR[:, j, :], QRE[:, j:j + 1], sA, ALU.mult, ALU.subtract)
                    kb.ts("dve", sB, BXR[:, j, :], QIM[:, j:j + 1], None, ALU.mult)
                    kb.stt(bt[1], BXI[:, j, :], QRE[:, j:j + 1], sB, ALU.mult, ALU.add)
                    p = ps()
                    pb = p.bitcast(BF16)
                    kb.tr(pb[:, 0:128], bt[0], ident_b)
                    kb.tr(pb[:, 128:256], bt[1], ident_b)
                    kb.cp("act", BRE[:, j, :], pb[:, 0:128])
                    kb.cp("act", BIM[:, j, :], pb[:, 128:256])
                kb.memset("pool", CRE, 0.0)
                kb.memset("pool", NCIM, 0.0)
                CXR = kb.sb(P0, "CXR", [128, 128], F32)
                CXI = kb.sb(P0, "CXI", [128, 128], F32)
                for cb in range(4):
                    kb.memset("pool", CXR, 0.0)
                    kb.memset("pool", CXI, 0.0)
                    for g8 in range(8):
                        g = cb * 8 + g8
                        c0 = (g8 % 2) * 64
                        kb.dma("sp", CXR[g8 * 16:(g8 + 1) * 16, c0:c0 + 64], c_re_d[g, :, :])
                        kb.dma("sp", CXI[g8 * 16:(g8 + 1) * 16, c0:c0 + 64], c_im_d[g, :, :])
                    p = ps()
                    kb.tr(p[:, 0:128], CXR, ident_f)
                    kb.tr(p[:, 128:256], CXI, ident_f)
                    for jj in range(4):
                        j = cb * 4 + jj
                        kb.cp("act", CRE[:, j, jj * 32:(jj + 1) * 32], p[:, jj * 32:(jj + 1) * 32])
                        kb.act(NCIM[:, j, jj * 32:(jj + 1) * 32], p[:, 128 + jj * 32:128 + (jj + 1) * 32], AF.Copy, scale=-1.0)
            kb.barrier()
            if stop_after == "P0":
                kb.finish()
                return nc, dbg_outs

            xt = kb.sb(A, "xt", [128, D], F32)
            hb = kb.sb(A, "hb", [128, D], BF16)
            scr = None
            ss = kb.sb(A, "ss", [128, 1], F32)
            rstd = kb.sb(A, "rstd", [128, 1], F32)
            hTx = [kb.sb(A, f"hTx{i}", [128, KC, 130], BF16) for i in range(2)]
            hP = kb.sb(A, "hP", [128, KC, 1], BF16)
            uT = kb.sb(A, "uT", [128, 4, 128], BF16)
            WL_RE = kb.sb(A, "WL_RE", [128, 16], F32)
            WL_IM = kb.sb(A, "WL_IM", [128, 16], F32)
            kb.memset("pool", XIN_RE, 0.0)
            kb.memset("pool", XIN_IM, 0.0)
            s5t = [kb.sb(A, f"s5t{i}", [128, 128], F32) for i in range(4)]
            s5z = [[kb.sb(A, f"s5z{i}_{q}", [128, 128], BF16) for q in range(4)] for i in range(3)]
            s5w = [(kb.sb(A, f"s5wr{i}", [128, 128], F32), kb.sb(A, f"s5wi{i}", [128, 128], F32)) for i in range(2)]
            s5c = [kb.sb(A, f"s5c{i}", [128, 16], F32) for i in range(2)]
            yT = kb.sb(A, "yT", [128, 2, 128], F32)
            txa = kb.sb(A, "txa", [128, 128], BF16)
            sxg = kb.sb(A, "sxg", [128, 128], BF16)
            sgw = kb.sb(A, "sgw", [128, 512], F32)
            asig = kb.sb(A, "asig", [128, 512], F32)
            g_sb = kb.sb(A, "g_sb", [128, 512], BF16)
            eW = kb.sb(A, "eW", [128, 512], BF16)
            eWi = kb.sb(A, "eWi", [128, 512], BF16)
            eWx = kb.sb(A, "eWx", [128, 512], BF16)
            eRem = kb.sb(A, "eRem", [128, 512], BF16)
            kkn = kb.sb(A, "kkn", [128, 512], F32)
            kmod = sgw
            bvec = kb.sb(A, "bvec", [128, 512], F32)
            tmpA = kb.sb(A, "tmpA", [128, 512], F32)
            st8 = [kb.sb(A, f"st8_{i}", [128, 8], F32) for i in range(3)]
            eTot = [kb.sb(A, f"eTot{i}", [64, 512], F32) for i in range(2)]
            At = [kb.sb(A, f"At{i}", [128, 512], BF16) for i in range(2)]
            Rt = [kb.sb(A, f"Rt{i}", [128, 512], BF16) for i in range(2)]
            Btl = [kb.sb(A, f"Btl{i}", [128, 512], BF16) for i in range(2)]
            Ktl = [kb.sb(A, f"Ktl{i}", [128, 512], BF16) for i in range(2)]
            Vb = [kb.sb(A, f"Vb{i}", [128, 512], BF16) for i in range(2)]
            Bh = [kb.sb(A, f"Bh{i}", [128, 512], BF16) for i in range(2)]
            Kh = [kb.sb(A, f"Kh{i}", [128, 512], BF16) for i in range(2)]
            bv = g_sb
            HGB = []
            for hg_ in range(2):
                HGB.append((
                    kb.sb(A, f"KM{hg_}", [64, 4, 512], BF16),
                    [kb.sb(A, f"S1_{hg_}_{h}", [128, 512], BF16) for h in range(4)],
                    [kb.sb(A, f"NTp{hg_}_{i}", [128, 4, 128], BF16) for i in range(2)],
                    [kb.sb(A, f"Np{hg_}_{i}", [128, 4, 128], BF16) for i in range(2)],
                    kb.sb(A, f"Gm{hg_}", [128, 4, 128], BF16),
                    kb.sb(A, f"U1_{hg_}", [128, 4, 64], BF16),
                    kb.sb(A, f"AhU{hg_}", [128, 4, 128], BF16),
                    kb.sb(A, f"RhT{hg_}", [64, 4, 128], BF16),
                    kb.sb(A, f"Pm{hg_}", [64, 4, 64], F32),
                ))
            MTb = [kb.sb(A, f"MTb{i}", [64, 4, 64], BF16) for i in range(2)]
            ZT = [kb.sb(A, f"ZT{i}", [64, 4, 128], BF16) for i in range(2)]
            Yl = [kb.sb(A, f"Yl{i}", [128, 256], F32) for i in range(2)]
            for hg_ in range(2):
                kb.memset("pool", ST[hg_], 0.0)
                kb.memset("pool", STb[hg_], 0.0)
                for h in range(4):
                    kb.cp("pool", MT[hg_][:, h, :], ident_f[0:64, 0:64])
                    kb.cp("pool", MTb[hg_][:, h, :], ident_f[0:64, 0:64])

            psa_rr = [0]

            def psA():
                b = psb[psa_rr[0]]
                psa_rr[0] = (psa_rr[0] + 1) % 7
                return b

            kb.dma("sp", xt[0:1, :], xprev_d)
            norm_tile(xt, 1, g1_bc, hb, scr, ss, rstd)
            transpose_rows(hP, hb, 1, KC)

            def front_gen(n):
                par = n % 2
                hTc = hTx[par][:, :, 1:129]
                kb.dma("sp", xt, x_d[n * 128:(n + 1) * 128, :])
                norm_tile(xt, 128, g1_bc, hb, scr, ss, rstd)
                yield
                p = psA()
                pb = p.bitcast(BF16)
                for c in range(KC):
                    kb.tr(pb[:, c * 128:(c + 1) * 128], hb[:, c * 128:(c + 1) * 128], ident_b)
                kb.cp("act", hTc, V(pb.ap.rearrange("p (c t) -> p c t", c=KC), p.buf))
                if n == 0:
                    kb.cp("pool", hTx[par][:, :, 0:1], hP)
                else:
                    kb.cp("pool", hTx[par][:, :, 0:1], hTx[1 - par][:, :, 128:129])
                yield
                p = psA()
                for cb in range(4):
                    for kc in range(KC):
                        kb.mm(p[:, cb * 128:(cb + 1) * 128], WIN[:, kc, cb * 128:(cb + 1) * 128], hTc[:, kc, :], start=(kc == 0), stop=(kc == KC - 1))
                kb.cp("act", uT, V(p.ap.rearrange("p (c t) -> p c t", c=4), p.buf))
                yield

            def s5_gen(n):
                pY = ps_long
                t1, t2, t3, t4 = s5t[0:4]

                def emit_out(j):
                    cb, jj = divmod(j, 4)
                    z1, z2, z3, z4 = s5z[j % 3]
                    kb.mm(pY[:, cb * 128:(cb + 1) * 128], CRE[:, j, :], z1, start=(jj == 0), stop=False)
                    kb.mm(pY[:, cb * 128:(cb + 1) * 128], CRE[:, j, :], z2, start=False, stop=False)
                    kb.mm(pY[:, cb * 128:(cb + 1) * 128], NCIM[:, j, :], z3, start=False, stop=False)
                    kb.mm(pY[:, cb * 128:(cb + 1) * 128], NCIM[:, j, :], z4, start=False, stop=(jj == 3))

                for j in range(16):
                    cb = j // 4
                    sl = j % 2
                    wre, wim = s5w[sl]
                    z1, z2, z3, z4 = s5z[j % 3]
                    cosj, sinj = COS[:, j, :], SIN[:, j, :]
                    pB = psA()
                    kb.mm(pB[:, 0:128], BRE[:, j, :], uT[:, cb, :])
                    kb.mm(pB[:, 128:256], BIM[:, j, :], uT[:, cb, :])
                    kb.tt("dve", t1, pB[:, 0:128], cosj, ALU.mult)
                    kb.tt("dve", t2, pB[:, 128:256], sinj, ALU.mult)
                    kb.tt("dve", t3, pB[:, 128:256], cosj, ALU.mult)
                    kb.tt("dve", t4, pB[:, 0:128], sinj, ALU.mult)
                    kb.tt("dve", t1, t1, t2, ALU.add)
                    kb.tt("dve", t3, t3, t4, ALU.subtract)
                    rho_b = V(RHO.ap[:, j:j + 1].broadcast_to([128, 128]), RHO.buf)
                    kb.scan(wre, rho_b, t1, XIN_RE[:, j:j + 1])
                    kb.scan(wim, rho_b, t3, XIN_IM[:, j:j + 1])
                    if j >= 2:
                        emit_out(j - 2)
                    yield
                    kb.cp("act", WL_RE[:, j:j + 1], wre[:, 127:128])
                    kb.cp("act", WL_IM[:, j:j + 1], wim[:, 127:128])
                    kb.tt("pool", z1, wre, cosj, ALU.mult)
                    kb.stt(z2, wim, -1.0, sinj, ALU.mult, ALU.mult)
                    kb.tt("pool", z3, wre, sinj, ALU.mult)
                    kb.tt("pool", z4, wim, cosj, ALU.mult)
                    yield
                emit_out(14)
                emit_out(15)
                c1 = s5c[0]
                c2 = s5c[1]
                kb.tt("pool", c1, WL_RE, R128C, ALU.mult)
                kb.tt("pool", c2, WL_IM, R128S, ALU.mult)
                kb.tt("pool", XIN_RE, c1, c2, ALU.subtract)
                kb.tt("pool", c1, WL_RE, R128S, ALU.mult)
                kb.tt("pool", c2, WL_IM, R128C, ALU.mult)
                kb.tt("pool", XIN_IM, c1, c2, ALU.add)
                for cb in range(4):
                    kb.stt(yT[:, cb % 2, :], uT[:, cb, :], DSK[:, cb:cb + 1], pY[:, cb * 128:(cb + 1) * 128], ALU.mult, ALU.add)
                    if cb % 2 == 1:
                        kb.dma("sp", rec_yT[n][:, (cb - 1) * 128:(cb + 1) * 128], V(yT.ap.rearrange("p c t -> p (c t)"), yT.buf))
                yield

            def prep_gen(n):
                par = n % 2
                hTc = hTx[par][:, :, 1:129]
                hTs = hTx[par][:, :, 0:128]
                pX = psA()
                for (c_w, c_s, o_) in ((2176, 1664, 0), (2048, 1536, 128)):
                    for kc in range(KC):
                        kb.mm(pX[:, o_:o_ + 128], WIN[:, kc, c_w:c_w + 128], hTc[:, kc, :], start=(kc == 0), stop=False)
                    for kc in range(KC):
                        kb.mm(pX[:, o_:o_ + 128], WSH[:, kc, c_s:c_s + 128], hTs[:, kc, :], start=False, stop=(kc == KC - 1))
                kb.act(txa[0:64, :], pX[0:64, 0:128], AF.Tanh)
                kb.cp("act", txa[64:128, :], pX[64:128, 0:128])
                kb.act(sxg, pX[:, 128:256], AF.Sigmoid)
                yield
                pW = psA()
                kb.mm(pW, txa[0:64, :], WA2[0:64, :])
                pA = psA()
                kb.mm(pA, txa[64:128, :], WA2[64:128, :])
                pG = psA()
                kb.mm(pG, sxg, G2B)
                kb.tt("dve", tmpA, pW, w0_bc, ALU.add)
                kb.tt("dve", bvec, pA, a0_bc, ALU.add)
                kb.cp("act", g_sb, pG)
                kb.act(sgw, tmpA, AF.Sigmoid)
                kb.act(asig, bvec, AF.Sigmoid)
                kb.dma("sp", rec_g[n], g_sb)
                yield
                pC = [psA() for _ in range(4)]
                for i in range(4):
                    kb.mm(pC[i], TRI[:, i, :], sgw)
                kb.act(eW, pC[0], AF.Exp)
                kb.act(eWi, pC[0], AF.Exp, scale=-1.0)
                kb.act(eWx, pC[1], AF.Exp)
                kb.act(eRem, pC[2], AF.Exp)
                kb.act(eTot[par], pC[3][0:64, :], AF.Exp)
                kb.tt("pool", V(eTot[par].ap.rearrange("p (h k) -> p h k", h=8), eTot[par].buf), V(eTot[par].ap.rearrange("p (h k) -> p h k", h=8), eTot[par].buf), V(I64R.ap.unsqueeze(1).broadcast_to([64, 8, 64]), I64R.buf), ALU.mult)
                kb.stt(kmod, asig, -1.0, ka_bc, ALU.add, ALU.mult)
                kb.ts("pool", kmod, kmod, 1.0, None, ALU.add)
                yield
                pR, pK, pV = psA(), psA(), psA()
                for (pp, o_) in ((pR, 0), (pK, 512), (pV, 1024)):
                    for kc in range(KC):
                        kb.mm(pp, hTc[:, kc, :], WIN[:, kc, 512 + o_:512 + o_ + 512], start=(kc == 0), stop=False)
                    for kc in range(KC):
                        kb.mm(pp, hTs[:, kc, :], WSH[:, kc, o_:o_ + 512], start=False, stop=(kc == KC - 1))
                kb.cp("act", Vb[par], pV)
                kb.tt("dve", Rt[par], pR, eW, ALU.mult)
                kb.tt("dve", tmpA, pR, rk_bc, ALU.mult)
                kb.tt("dve", kkn, pK, kk_bc, ALU.mult)
                kb.tt("dve", kmod, pK, kmod, ALU.mult)
                yield
                kb.tt("pool", bvec, kkn, kkn, ALU.mult)
                kb.red(st8[0], V(bvec.ap.rearrange("p (h k) -> p h k", h=8), bvec.buf))
                kb.act(st8[0], st8[0], AF.Sqrt)
                kb.ts("dve", st8[0], st8[0], 1e-12, None, ALU.max)
                kb.recip(st8[0], st8[0])
                yield
                kk3 = V(kkn.ap.rearrange("p (h k) -> p h k", h=8), kkn.buf)
                kb.tt("pool", kk3, kk3, V(st8[0].ap.unsqueeze(2).broadcast_to([128, 8, 64]), st8[0].buf), ALU.mult)
                kb.tt("pool", bvec, kkn, asig, ALU.mult)
                yield
                kb.tt("dve", tmpA, tmpA, kmod, ALU.mult)
                kb.red(st8[1], V(tmpA.ap.rearrange("p (h k) -> p h k", h=8), tmpA.buf))
                kb.tt("pool", V(bv.ap.rearrange("p (h k) -> p h k", h=8), bv.buf), V(Vb[par].ap.rearrange("p (h k) -> p h k", h=8), Vb[par].buf),
                      V(st8[1].ap.unsqueeze(2).broadcast_to([128, 8, 64]), st8[1].buf), ALU.mult)
                kb.dma("sp", rec_bv[n], bv)
                yield
                kb.tt("dve", Ktl[par], kmod, eWi, ALU.mult)
                kb.tt("pool", Btl[par], bvec, eWi, ALU.mult)
                kb.stt(At[par], kkn, -1.0, eWx, ALU.mult, ALU.mult)
                kb.tt("pool", Kh[par], kmod, eRem, ALU.mult)
                kb.tt("dve", Bh[par], bvec, eRem, ALU.mult)
                yield

            def fsp_gen(n):
                yield from front_gen(n)
                g5, gp = s5_gen(n), prep_gen(n)
                alive = [g5, gp]
                while alive:
                    for g_ in list(alive):
                        try:
                            next(g_)
                            yield
                        except StopIteration:
                            alive.remove(g_)

            def rw_gen(n, hg):
                par = n % 2
                KM, S1, NTp, Np, Gm, U1, AhU, RhT, Pm = HGB[hg]
                At_, Rt_, Btl_, Ktl_, Vb_, Bh_, Kh_, eTot_ = At[par], Rt[par], Btl[par], Ktl[par], Vb[par], Bh[par], Kh[par], eTot[par]
                hs = [hg * 4 + i for i in range(4)]
                for pair in range(2):
                    p = psA()
                    pb = p.bitcast(BF16)
                    for i in range(2):
                        hl = pair * 2 + i
                        h = hs[hl]
                        for q, src in enumerate((At_, Rt_, Btl_, Ktl_)):
                            kb.tr(pb[0:64, (i * 4 + q) * 128:(i * 4 + q + 1) * 128], src[:, h * 64:(h + 1) * 64], ident_b)
                    kb.cp("act", KM[:, pair * 2:pair * 2 + 2, :], V(pb.ap[0:64, :].rearrange("p (h c) -> p h c", h=2), p.buf))
                    yield
                pY1 = psA()
                for hl in range(4):
                    p = psA()
                    kb.mm(p[:, 0:256], KM[:, hl, 256:384], KM[:, hl, 0:256])
                    kb.mm(p[:, 256:512], KM[:, hl, 384:512], KM[:, hl, 0:256])
                    kb.mm(pY1[:, hl * 128:(hl + 1) * 128], KM[:, hl, 0:128], KM[:, hl, 256:384])
                    kb.tt("dve", S1[hl], p, MASK4, ALU.mult)
                kb.tt("dve", NTp[0], V(pY1.ap.rearrange("p (h t) -> p h t", h=4), pY1.buf), V(ML.ap.unsqueeze(1).broadcast_to([128, 4, 128]), ML.buf), ALU.mult)
                yield
                p = psA()
                for hl in range(4):
                    h = hs[hl]
                    kb.mm(p[:, hl * 64:(hl + 1) * 64], S1[hl][:, 256:384], Vb_[:, h * 64:(h + 1) * 64])
                kb.cp("act", U1, V(p.ap[:, 0:256].rearrange("p (h v) -> p h v", h=4), p.buf))
                for hl in range(4):
                    kb.tt("pool", Gm[:, hl, :], S1[hl][:, 0:128], ident_b, ALU.add)
                yield
                cur = 0
                for lev in range(1, 8):
                    last = (lev == 7)
                    Ncur = [(S1[hl][:, 0:128] if lev == 1 else Np[cur][:, hl, :]) for hl in range(4)]
                    NTcur = [NTp[cur][:, hl, :] for hl in range(4)]
                    nxt = 1 - cur
                    if not last:
                        pN = psA()
                        pT = psA()
                        for hl in range(4):
                            kb.mm(pN[:, hl * 128:(hl + 1) * 128], NTcur[hl], Ncur[hl])
                            kb.mm(pT[:, hl * 128:(hl + 1) * 128], Ncur[hl], NTcur[hl])
                    if lev >= 2:
                        pGm = psA()
                        for hl in range(4):
                            kb.mm(pGm[:, hl * 128:(hl + 1) * 128], NTcur[hl], Gm[:, hl, :])
                        gmf = V(Gm.ap.rearrange("p h t -> p (h t)"), Gm.buf)
                        kb.tt("dve", gmf, pGm, gmf, ALU.add)
                    if not last:
                        kb.cp("act", V(Np[nxt].ap.rearrange("p h t -> p (h t)"), Np[nxt].buf), pN)
                        kb.cp("act", V(NTp[nxt].ap.rearrange("p h t -> p (h t)"), NTp[nxt].buf), pT)
                    cur = nxt
                    yield
                p = psA()
                for hl in range(4):
                    h = hs[hl]
                    kb.mm(p[:, hl * 128:hl * 128 + 64], Gm[:, hl, :], At_[:, h * 64:(h + 1) * 64])
                    kb.mm(p[:, hl * 128 + 64:hl * 128 + 128], Gm[:, hl, :], U1[:, hl, :])
                kb.cp("act", V(AhU.ap.rearrange("p h c -> p (h c)"), AhU.buf), p)
                yield
                pRh = psA()
                pP = psA()
                for hl in range(4):
                    h = hs[hl]
                    kb.mm(pRh[0:64, hl * 128:(hl + 1) * 128], AhU[:, hl, 0:64], S1[hl][:, 128:256], start=True, stop=False)
                    kb.mm(pRh[0:64, hl * 128:(hl + 1) * 128], Rt_[:, h * 64:(h + 1) * 64], ident_b, start=False, stop=True)
                    kb.mm(pP[0:64, hl * 64:(hl + 1) * 64], AhU[:, hl, 0:64], Bh_[:, h * 64:(h + 1) * 64])
                kb.cp("act", V(RhT.ap.rearrange("p h t -> p (h t)"), RhT.buf), pRh[0:64, :])
                kb.tt("dve", V(Pm.ap.rearrange("p h k -> p (h k)"), Pm.buf), pP[0:64, 0:256], eTot_[:, hg * 256:(hg + 1) * 256], ALU.add)
                yield
                pYl = psA()
                pZ = psA()
                for hl in range(4):
                    h = hs[hl]
                    kb.mm(pYl[:, hl * 64:(hl + 1) * 64], S1[hl][:, 128:256], AhU[:, hl, 64:128], start=True, stop=False)
                    kb.mm(pYl[:, hl * 64:(hl + 1) * 64], S1[hl][:, 384:512], Vb_[:, h * 64:(h + 1) * 64], start=False, stop=False)
                    kb.mm(pYl[:, hl * 64:(hl + 1) * 64], RhT[:, hl, :], STb[hg][:, hl, :], start=False, stop=True)
                    kb.mm(pZ[0:64, hl * 128:(hl + 1) * 128], MTb[hg][:, hl, :], RhT[:, hl, :])
                kb.cp("act", Yl[hg], pYl[:, 0:256])
                kb.cp("act", V(ZT[hg].ap.rearrange("p h t -> p (h t)"), ZT[hg].buf), pZ[0:64, :])
                kb.dma("sp", rec_Y[n][:, hg * 256:(hg + 1) * 256], Yl[hg])
                kb.dma("sp", rec_ZT[n][:, hg * 512:(hg + 1) * 512], V(ZT[hg].ap.rearrange("p h t -> p (h t)"), ZT[hg].buf))
                yield
                pS = psA()
                pM = psA()
                for hl in range(4):
                    h = hs[hl]
                    kb.mm(pS[0:64, hl * 64:(hl + 1) * 64], Bh_[:, h * 64:(h + 1) * 64], AhU[:, hl, 64:128], start=True, stop=False)
                    kb.mm(pS[0:64, hl * 64:(hl + 1) * 64], Kh_[:, h * 64:(h + 1) * 64], Vb_[:, h * 64:(h + 1) * 64], start=False, stop=False)
                    kb.mm(pS[0:64, hl * 64:(hl + 1) * 64], Pm[:, hl, :], ST[hg][:, hl, :], start=False, stop=True)
                    kb.mm(pM[0:64, hl * 64:(hl + 1) * 64], Pm[:, hl, :], MT[hg][:, hl, :])
                kb.cp("act", V(ST[hg].ap.rearrange("p h k -> p (h k)"), ST[hg].buf), pS[0:64, 0:256])
                kb.cp("act", V(STb[hg].ap.rearrange("p h k -> p (h k)"), STb[hg].buf), pS[0:64, 0:256])
                kb.cp("act", V(MT[hg].ap.rearrange("p h k -> p (h k)"), MT[hg].buf), pM[0:64, 0:256])
                kb.cp("act", V(MTb[hg].ap.rearrange("p h k -> p (h k)"), MTb[hg].buf), pM[0:64, 0:256])
                yield

            def drive(gens, weights):
                gens = list(gens)
                weights = list(weights)
                while gens:
                    dead = []
                    for idx in range(len(gens)):
                        for _ in range(weights[idx]):
                            try:
                                next(gens[idx])
                            except StopIteration:
                                dead.append(idx)
                                break
                    for idx in reversed(dead):
                        gens.pop(idx)
                        weights.pop(idx)

            drive([fsp_gen(0)], [1])
            for n in range(NT):
                gens = [rw_gen(n, 0), rw_gen(n, 1)]
                wts = [1, 1]
                if n + 1 < NT:
                    if FSP_FIRST:
                        gens.insert(0, fsp_gen(n + 1))
                        wts.insert(0, FSP_W)
                    else:
                        gens.append(fsp_gen(n + 1))
                        wts.append(FSP_W)
                drive(gens, wts)
            kb.barrier()
            if stop_after == "A":
                kb.finish()
                return nc, dbg_outs

        SUMM = kb.sb(G, "SUMM", [128, 1056], F32)
        kb.memset("pool", SUMM, 0.0)
        p = ps()
        for h in range(8):
            kb.tr(p[0:64, h * 64:(h + 1) * 64], MT[h // 4][:, h % 4, :], ident_f[0:64, 0:64])
        kb.cp("act", SUMM[0:64, 0:512], p[0:64, :])
        for hg_ in range(2):
            kb.cp("dve", SUMM[0:64, 512 + hg_ * 256:512 + (hg_ + 1) * 256], V(ST[hg_].ap.rearrange("p h k -> p (h k)"), ST[hg_].buf))
        kb.cp("dve", SUMM[:, 1024:1040], XIN_RE)
        kb.cp("dve", SUMM[:, 1040:1056], XIN_IM)
        kb.dma("sp", summ_d, SUMM)

        kb.memset("pool", STIN, 0.0)
        kb.memset("pool", XC_RE, 0.0)
        kb.memset("pool", XC_IM, 0.0)
        kb.memset("pool", gn_eps, GN_EPS)
        kb.dma("sp", FM, bcast(fmask_h, 8))
        LSR = kb.sb(G, "LSR", [128, 16], F32)
        LSI = kb.sb(G, "LSI", [128, 16], F32)
        L128R = kb.sb(G, "L128R", [128, 16], F32)
        L128I = kb.sb(G, "L128I", [128, 16], F32)
        XI_R = kb.sb(G, "XI_R", [128, 16], F32)
        XI_I = kb.sb(G, "XI_I", [128, 16], F32)
        XI_NI = kb.sb(G, "XI_NI", [128, 16], F32)
        cdummy = kb.sb(G, "cdummy", [128, 1], F32)
        B1 = ExitStack()
        WG = kb.sb(B1, "WG", [128, KC, 2048], BF16)
        WBR = kb.sb(B1, "WBR", [128, KC, D], BF16)
        WO = kb.sb(B1, "WO", [128, KC, D], BF16)
        WGL = kb.sb(B1, "WGL", [128, 4, 512], BF16)
        BGL = kb.sb(B1, "BGL", [128, 4], F32)
        lnxg_bc = kb.sb(B1, "lnxg_bc", [128, 512], F32)
        lnxb_bc = kb.sb(B1, "lnxb_bc", [128, 512], F32)
        kb.dma("pool", WG, V(w_in_d.ap[:, 2304:4352].rearrange("(kc p) c -> p kc c", p=128), w_in_d.buf))
        kb.dma("pool", WBR, V(w_br_d.ap.rearrange("(kc p) c -> p kc c", p=128), w_br_d.buf))
        kb.dma("pool", WO, V(w_out_d.ap.rearrange("(kc p) c -> p kc c", p=128), w_out_d.buf))
        kb.dma("pool", WGL, V(w_glu_d.ap.rearrange("(kc p) c -> p kc c", p=128), w_glu_d.buf))
        kb.dma("sp", BGL, V(bass.AP(tensor=b_glu_h, offset=0, ap=[[1, 128], [128, 4]]), b_glu_d.buf), allow_slow_non_contiguous=True)
        kb.dma("sp", lnxg_bc, bcast(lnx_g_h, 512))
        kb.dma("sp", lnxb_bc, bcast(lnx_b_h, 512))
        with B1:
            cc_sem_x = kb.allgather_issue(gath_d, summ_d, [[0, 1, 2, 3], [4, 5, 6, 7]]) if do_exchange else None
            if do_exchange:
                ER = kb.sb(B1, "ER", [128, 16, 128], BF16)
                EI = kb.sb(B1, "EI", [128, 16, 128], BF16)
                L1 = kb.sb(B1, "L1", [128, 16, 128], BF16)
                L2 = kb.sb(B1, "L2", [128, 16, 128], BF16)
                rp = kb.sb(B1, "rp", [128, 128], F32)
                xq = [kb.sb(B1, f"xq{i}", [128, 16], F32) for i in range(4)]
                kb.memset("pool", L1, 0.0)
                kb.memset("pool", L2, 0.0)
                for j in range(16):
                    kb.act(rp, TP1, AF.Exp, scale=LNA[:, j:j + 1])
                    kb.tt("dve", ER[:, j, :], rp, COS[:, j, :], ALU.mult)
                    kb.tt("pool", EI[:, j, :], rp, SIN[:, j, :], ALU.mult)

            xtB = [kb.sb(B1, f"xtB{i}", [128, D], F32) for i in range(2)]
            hbB = kb.sb(B1, "hbB", [128, D], BF16)
            ssB = kb.sb(B1, "ssB", [128, 1], F32)
            rstdB = kb.sb(B1, "rstdB", [128, 1], F32)
            hTB = kb.sb(B1, "hTB", [128, KC, 128], BF16)
            gate = [[kb.sb(B1, f"gate{i}_{par}", [128, D], BF16) for i in range(2)] for par in range(2)]
            yTl = kb.sb(B1, "yTl", [128, 512], F32)
            ga = kb.sb(B1, "ga", [128, 512], F32)
            gb_ = kb.sb(B1, "gb_", [128, 512], F32)
            zT = kb.sb(B1, "zT", [128, 4, 128], BF16)
            sg2 = kb.sb(B1, "sg2", [128, 4, 128], F32)
            oaT = [kb.sb(B1, f"oaT{i}", [128, 4, 128], BF16) for i in range(2)]
            YlB = kb.sb(B1, "YlB", [128, 512], F32)
            ZTl = kb.sb(B1, "ZTl", [64, 8, 128], BF16)
            gl_ = kb.sb(B1, "gl_", [128, 512], BF16)
            bvl = kb.sb(B1, "bvl", [128, 512], BF16)
            Yn = kb.sb(B1, "Yn", [128, 512], F32)
            sqB = kb.sb(B1, "sqB", [128, 512], F32)
            q8 = [kb.sb(B1, f"q8_{i}", [128, 8], F32) for i in range(4)]
            ob = kb.sb(B1, "ob", [128, 512], BF16)
            obT = [kb.sb(B1, f"obT{i}", [128, 4, 128], BF16) for i in range(2)]
            mg = kb.sb(B1, "mg", [128, D], F32)
            mg2 = kb.sb(B1, "mg2", [128, 512], F32)
            mgb = kb.sb(B1, "mgb", [128, D], BF16)
            mT = kb.sb(B1, "mT", [128, KC, 128], BF16)
            x1 = kb.sb(B1, "x1", [128, D], F32)
            if do_exchange:
                Ltmp = kb.sb(B1, "Ltmp", [128, 16, 128], BF16)
                XR_b = kb.sb(B1, "XR_b", [128, 16], BF16)

            def b1_front(n):
                par = n % 2
                xt_ = xtB[par]
                kb.dma("sp", xt_, x_d[n * 128:(n + 1) * 128, :])
                norm_tile(xt_, 128, g1_bc, hbB, None, ssB, rstdB)
                yield
                p = psA()
                pb = p.bitcast(BF16)
                for c in range(KC):
                    kb.tr(pb[:, c * 128:(c + 1) * 128], hbB[:, c * 128:(c + 1) * 128], ident_b)
                kb.cp("act", hTB, V(pb.ap.rearrange("p (c t) -> p c t", c=KC), p.buf))
                yield
                for half in range(2):
                    for cg in range(2):
                        p = psA()
                        c0 = half * 1024 + cg * 512
                        for kc in range(KC):
                            kb.mm(p, hTB[:, kc, :], WG[:, kc, c0:c0 + 512], start=(kc == 0), stop=(kc == KC - 1))
                        kb.act(gate[par][half][:, cg * 512:(cg + 1) * 512], p, AF.Sigmoid)
                        yield

            def b1_s5(n):
                par = n % 2
                kb.dma("sp", yTl, rec_yT[n])
                if do_exchange:
                    C3, N3 = CRE, NCIM
                    XR3 = V(XI_R.ap.unsqueeze(2).broadcast_to([128, 16, 128]), XI_R.buf)
                    XI3 = V(XI_I.ap.unsqueeze(2).broadcast_to([128, 16, 128]), XI_I.buf)
                    XN3 = V(XI_NI.ap.unsqueeze(2).broadcast_to([128, 16, 128]), XI_NI.buf)
                    kb.tt("dve", L1, C3, XR3, ALU.mult)
                    kb.tt("pool", Ltmp, N3, XI3, ALU.mult)
                    kb.tt("dve", L1, L1, Ltmp, ALU.add)
                    yield
                    kb.tt("pool", L2, N3, XR3, ALU.mult)
                    kb.tt("dve", Ltmp, C3, XN3, ALU.mult)
                    kb.tt("pool", L2, L2, Ltmp, ALU.add)
                    kb.tt("pool", xq[0], L128R, XI_R, ALU.mult)
                    kb.tt("pool", xq[1], L128I, XI_I, ALU.mult)
                    kb.tt("pool", xq[2], L128R, XI_I, ALU.mult)
                    kb.tt("pool", xq[3], L128I, XI_R, ALU.mult)
                    yield
                    kb.tt("pool", XI_R, xq[0], xq[1], ALU.subtract)
                    kb.tt("pool", XI_I, xq[2], xq[3], ALU.add)
                    kb.ts("pool", XI_NI, XI_I, -1.0, None, ALU.mult)
                    pc = psA()
                    for j in range(16):
                        cb, jj = divmod(j, 4)
                        kb.mm(pc[:, cb * 128:(cb + 1) * 128], L1[:, j, :], ER[:, j, :], start=(jj == 0), stop=False)
                        kb.mm(pc[:, cb * 128:(cb + 1) * 128], L2[:, j, :], EI[:, j, :], start=False, stop=(jj == 3))
                    kb.tt("dve", yTl, pc, yTl, ALU.add)
                    yield
                kb.tt("pool", ga, yTl, yTl, ALU.mult)
                kb.ts("pool", ga, ga, 0.044715, 1.0, ALU.mult, ALU.add)
                kb.tt("pool", ga, ga, yTl, ALU.mult)
                kb.act(gb_, ga, AF.Sigmoid, scale=1.5957691216057308)
                kb.tt("dve", V(zT.ap.rearrange("p c t -> p (c t)"), zT.buf), yTl, gb_, ALU.mult)
                yield
                p = psA()
                for ob_ in range(4):
                    for kc in range(4):
                        kb.mm(p[:, ob_ * 128:(ob_ + 1) * 128], WGL[:, kc, ob_ * 128:(ob_ + 1) * 128], zT[:, kc, :], start=(kc == 0), stop=(kc == 3))
                for ob_ in range(4):
                    kb.act(sg2[:, ob_, :], p[:, ob_ * 128:(ob_ + 1) * 128], AF.Sigmoid, bias=BGL[:, ob_:ob_ + 1])
                kb.tt("dve", oaT[par], zT, sg2, ALU.mult)
                if n == NT - 1:
                    dbg_out("oaT", V(oaT[par].ap.rearrange("p c t -> p (c t)"), oaT[par].buf), [128, 512], BF16)
                yield

            def b1_rw(n):
                par = n % 2
                kb.dma("sp", YlB, rec_Y[n])
                kb.dma("sp", V(ZTl.ap.rearrange("p h t -> p (h t)"), ZTl.buf), rec_ZT[n])
                kb.dma("sp", gl_, rec_g[n])
                kb.dma("sp", bvl, rec_bv[n])
                p = psA()
                for h in range(8):
                    kb.mm(p[:, h * 64:(h + 1) * 64], ZTl[:, h, :], STIN[:, h, :])
                kb.tt("dve", YlB, p, YlB, ALU.add)
                yield
                Y3 = V(YlB.ap.rearrange("p (h k) -> p h k", h=8), YlB.buf)
                kb.red(q8[0], Y3)
                kb.tt("pool", sqB, YlB, YlB, ALU.mult)
                kb.red(q8[1], V(sqB.ap.rearrange("p (h k) -> p h k", h=8), sqB.buf))
                kb.ts("dve", q8[0], q8[0], 1.0 / 64, None, ALU.mult)
                kb.tt("dve", q8[2], q8[0], q8[0], ALU.mult)
                kb.stt(q8[1], q8[1], 1.0 / 64, q8[2], ALU.mult, ALU.subtract)
                kb.act(q8[1], q8[1], AF.Sqrt, bias=gn_eps)
                kb.recip(q8[1], q8[1])
                yield
                Yn3 = V(Yn.ap.rearrange("p (h k) -> p h k", h=8), Yn.buf)
                kb.tt("dve", Yn3, Y3, V(q8[0].ap.unsqueeze(2).broadcast_to([128, 8, 64]), q8[0].buf), ALU.subtract)
                kb.tt("pool", Yn3, Yn3, V(q8[1].ap.unsqueeze(2).broadcast_to([128, 8, 64]), q8[1].buf), ALU.mult)
                yield
                kb.tt("pool", Yn, Yn, lnxg_bc, ALU.mult)
                kb.tt("pool", Yn, Yn, lnxb_bc, ALU.add)
                kb.tt("dve", Yn, Yn, bvl, ALU.add)
                kb.tt("dve", ob, Yn, gl_, ALU.mult)
                if n == NT - 1:
                    dbg_out("ob", ob, [128, 512], BF16)
                yield
                p = psA()
                pb = p.bitcast(BF16)
                for c in range(4):
                    kb.tr(pb[:, c * 128:(c + 1) * 128], ob[:, c * 128:(c + 1) * 128], ident_b)
                kb.cp("act", obT[par], V(pb.ap[:, 0:512].rearrange("p (c t) -> p c t", c=4), p.buf))
                yield

            def b1_join(n):
                par = n % 2
                for half in range(2):
                    src = oaT[par] if half == 0 else obT[par]
                    for cg in range(2):
                        p = psA()
                        for kc in range(4):
                            kb.mm(p, src[:, kc, :], WBR[:, half * 4 + kc, cg * 512:(cg + 1) * 512], start=(kc == 0), stop=(kc == 3))
                        if half == 0:
                            kb.tt("dve", mg[:, cg * 512:(cg + 1) * 512], p, gate[par][0][:, cg * 512:(cg + 1) * 512], ALU.mult)
                        else:
                            kb.tt("dve", mg2, p, gate[par][1][:, cg * 512:(cg + 1) * 512], ALU.mult)
                            kb.tt("pool", mgb[:, cg * 512:(cg + 1) * 512], mg[:, cg * 512:(cg + 1) * 512], mg2, ALU.add)
                        yield
                p = psA()
                pb = p.bitcast(BF16)
                for c in range(KC):
                    kb.tr(pb[:, c * 128:(c + 1) * 128], mgb[:, c * 128:(c + 1) * 128], ident_b)
                kb.cp("act", mT, V(pb.ap.rearrange("p (c t) -> p c t", c=KC), p.buf))
                yield
                for cg in range(2):
                    p = psA()
                    for kc in range(KC):
                        kb.mm(p, mT[:, kc, :], WO[:, kc, cg * 512:(cg + 1) * 512], start=(kc == 0), stop=(kc == KC - 1))
                    kb.tt("dve", x1[:, cg * 512:(cg + 1) * 512], p, xtB[par][:, cg * 512:(cg + 1) * 512], ALU.add)
                    yield
                kb.dma("sp", rec_x1[n], x1)
                if n == NT - 1:
                    dbg_out("x1", x1, [128, D])
                yield

            for _ in b1_front(0):
                pass
            if do_exchange:
                with ExitStack() as X:
                    GA = kb.sb(X, "GA", [128, 4, 1056], F32)
                    STf = kb.sb(X, "STf", [64, 8, 64], F32)
                    dfx = kb.sb(X, "dfx", [64, 512], F32)
                    e1 = [kb.sb(X, f"e1_{i}", [128, 16], F32) for i in range(4)]
                    kb.allgather_wait(cc_sem_x, gath_d, summ_d, cdummy)
                    kb.dma("sp", GA, V(gath_d.ap.rearrange("(r p) c -> p r c", p=128), gath_d.buf))
                    kb.cp("dve", LSR, ABRE)
                    kb.cp("dve", LSI, ABIM)
                    nsq = int(round(math.log2(NT * 128)))
                    assert 2 ** nsq == NT * 128
                    for i in range(nsq):
                        kb.tt("dve", e1[0], LSR, LSR, ALU.mult)
                        kb.tt("dve", e1[1], LSI, LSI, ALU.mult)
                        kb.tt("dve", e1[2], LSR, LSI, ALU.mult)
                        kb.tt("dve", LSR, e1[0], e1[1], ALU.subtract)
                        kb.ts("dve", LSI, e1[2], 2.0, None, ALU.mult)
                        if i == 6:
                            kb.cp("dve", L128R, LSR)
                            kb.cp("dve", L128I, LSI)
                    kb.memset("pool", STf, 0.0)
                    kb.memset("pool", XC_RE, 0.0)
                    kb.memset("pool", XC_IM, 0.0)
                    STf2 = V(STf.ap.rearrange("p h k -> p (h k)"), STf.buf)
                    for r in range(3):
                        p = ps()
                        for h in range(8):
                            kb.mm(p[0:64, h * 64:(h + 1) * 64], GA[0:64, r, h * 64:(h + 1) * 64], STf[:, h, :])
                        kb.tt("dve", dfx, p[0:64, :], GA[0:64, r, 512:1024], ALU.add)
                        kb.tt("dve", dfx, dfx, STf2, ALU.subtract)
                        kb.stt(STf2, dfx, FM[0:64, r:r + 1], STf2, ALU.mult, ALU.add)
                        kb.tt("dve", e1[0], LSR, XC_RE, ALU.mult)
                        kb.tt("dve", e1[1], LSI, XC_IM, ALU.mult)
                        kb.tt("dve", e1[0], e1[0], e1[1], ALU.subtract)
                        kb.tt("dve", e1[0], e1[0], GA[:, r, 1024:1040], ALU.add)
                        kb.tt("dve", e1[2], LSR, XC_IM, ALU.mult)
                        kb.tt("dve", e1[3], LSI, XC_RE, ALU.mult)
                        kb.tt("dve", e1[2], e1[2], e1[3], ALU.add)
                        kb.tt("dve", e1[2], e1[2], GA[:, r, 1040:1056], ALU.add)
                        kb.tt("dve", e1[0], e1[0], XC_RE, ALU.subtract)
                        kb.tt("dve", e1[2], e1[2], XC_IM, ALU.subtract)
                        kb.stt(XC_RE, e1[0], FM[:, r:r + 1], XC_RE, ALU.mult, ALU.add)
                        kb.stt(XC_IM, e1[2], FM[:, r:r + 1], XC_IM, ALU.mult, ALU.add)
                    kb.cp("dve", V(STIN.ap.rearrange("p h k -> p (h k)"), STIN.buf), STf2)
            kb.cp("dve", XI_R, XC_RE)
            kb.cp("dve", XI_I, XC_IM)
            kb.ts("dve", XI_NI, XC_IM, -1.0, None, ALU.mult)

            for n in range(NT + 1):
                if B1SEQ:
                    if n < NT:
                        for g_ in ((b1_front(n),) if n else ()) + (b1_s5(n), b1_rw(n), b1_join(n)):
                            for _ in g_:
                                pass
                    continue
                gens, wts = [], []
                if n >= 1:
                    gens.append(b1_join(n - 1))
                    wts.append(1)
                if n < NT:
                    if n >= 1:
                        gens.append(b1_front(n))
                        wts.append(1)
                    gens += [b1_s5(n), b1_rw(n)]
                    wts += [2, 2]
                drive(gens, wts)
            kb.barrier()
            if stop_after == "B1":
                kb.finish()
                return nc, dbg_outs

        with ExitStack() as B2:
            W1b = [kb.sb(B2, f"W1b{i}", [128, KC, 512], BF16) for i in range(8)]
            W2b = [kb.sb(B2, f"W2b{i}", [128, 4, D], BF16) for i in range(8)]
            g2_bc = kb.sb(B2, "g2_bc", [128, D], F32)
            gf_bc = kb.sb(B2, "gf_bc", [128, D], F32)
            for f4 in range(8):
                kb.dma("pool", W1b[f4], V(w_ff1_d.ap[:, f4 * 512:(f4 + 1) * 512].rearrange("(kc p) c -> p kc c", p=128), w_ff1_d.buf))
            for f4 in range(8):
                kb.dma("pool", W2b[f4], V(w_ff2_d.ap[f4 * 512:(f4 + 1) * 512, :].rearrange("(kc p) c -> p kc c", p=128), w_ff2_d.buf))
            kb.dma("sp", g2_bc, bcast(norm2_h, D))
            kb.dma("sp", gf_bc, bcast(normf_h, D))
            x1t = [kb.sb(B2, f"x1t{i}", [128, D], F32) for i in range(3)]
            hbC = kb.sb(B2, "hbC", [128, D], BF16)
            ssC = [kb.sb(B2, f"ssC{i}", [128, 1], F32) for i in range(2)]
            rstdC = [kb.sb(B2, f"rstdC{i}", [128, 1], F32) for i in range(2)]
            h2T = [kb.sb(B2, f"h2T{i}", [128, KC, 128], BF16) for i in range(2)]
            rl = [kb.sb(B2, f"rl{i}", [128, 512], BF16) for i in range(2)]
            f1T = [kb.sb(B2, f"f1T{i}", [128, 32, 128], BF16) for i in range(2)]
            junkC = kb.sb(B2, "junkC", [128, D], BF16)

            def b2_front(n):
                par = n % 2
                kb.dma("sp", x1t[n % 3], rec_x1[n])
                norm_tile(x1t[n % 3], 128, g2_bc, hbC, None, ssC[0], rstdC[0])
                yield
                p = psA()
                pb = p.bitcast(BF16)
                for c in range(KC):
                    kb.tr(pb[:, c * 128:(c + 1) * 128], hbC[:, c * 128:(c + 1) * 128], ident_b)
                kb.cp("act", h2T[par], V(pb.ap.rearrange("p (c t) -> p c t", c=KC), p.buf))
                yield

            def b2_ffn1(n):
                par = n % 2
                for f4 in range(8):
                    p = psA()
                    for i in range(4):
                        fb = f4 * 4 + i
                        for kc in range(KC):
                            kb.mm(p[:, i * 128:(i + 1) * 128], W1b[f4][:, kc, i * 128:(i + 1) * 128], h2T[par][:, kc, :], start=(kc == 0), stop=(kc == KC - 1))
                    r_ = rl[f4 % 2]
                    kb.act(r_, p, AF.Relu)
                    kb.tt("pool", V(f1T[par].ap[:, f4 * 4:(f4 + 1) * 4, :].rearrange("p c t -> p (c t)"), f1T[par].buf), r_, r_, ALU.mult)
                    yield

            def b2_ffn2(n):
                par = n % 2
                for cg in range(2):
                    p = psA()
                    for fb in range(32):
                        kb.mm(p, f1T[par][:, fb, :], W2b[fb // 4][:, fb % 4, cg * 512:(cg + 1) * 512], start=(fb == 0), stop=(fb == 31))
                    kb.tt("dve", x1t[n % 3][:, cg * 512:(cg + 1) * 512], p, x1t[n % 3][:, cg * 512:(cg + 1) * 512], ALU.add)
                    yield
                norm_tile(x1t[n % 3], 128, gf_bc, x1t[n % 3], junkC, ssC[1], rstdC[1])
                kb.dma("sp", out_d[n * 128:(n + 1) * 128, :], x1t[n % 3])
                yield

            for n in range(NT + 2):
                if B2SEQ:
                    if n < NT:
                        for g_ in (b2_front(n), b2_ffn1(n), b2_ffn2(n)):
                            for _ in g_:
                                pass
                    continue
                gens, wts = [], []
                if 0 <= n - 2 < NT:
                    gens.append(b2_ffn2(n - 2))
                    wts.append(1)
                if 0 <= n - 1 < NT:
                    gens.append(b2_ffn1(n - 1))
                    wts.append(3)
                if n < NT:
                    gens.append(b2_front(n))
                    wts.append(1)
                drive(gens, wts)

        kb.finish()
    return nc, dbg_outs


_W_KEYS = ["norm1_g", "w_in", "lam_re", "lam_im", "log_dt", "b_re", "b_im", "c_re", "c_im", "d_skip", "w_glu", "b_glu",
           "mu_rwkv", "w0", "w2", "a0", "a2", "g2", "k_k", "k_a", "r_k", "lnx_g", "lnx_b", "w_branch", "w_out",
           "norm2_g", "w_ff1", "w_ff2"]


def _prep_weights(inputs):
    w = {}
    for k in _W_KEYS:
        a = np.asarray(inputs[k], dtype=np.float32)[0]
        if k == "r_k":
            a = a.reshape(D_RWKV)
        w[k] = np.ascontiguousarray(a)
    w["norm_f_g"] = np.ascontiguousarray(np.asarray(inputs["norm_f_g"], dtype=np.float32))
    return w


def make_in_maps(inputs, NT, n_cores=8):
    x = np.asarray(inputs["x"], dtype=np.float32)
    w = _prep_weights(inputs)
    seg = NT * TOK
    maps = []
    for c in range(n_cores):
        b, j = divmod(c, NSEG)
        m = dict(w)
        m["x"] = np.ascontiguousarray(x[b, j * seg:(j + 1) * seg])
        if j > 0:
            m["xprev"] = np.ascontiguousarray(x[b, j * seg - 1:j * seg])
        else:
            m["xprev"] = np.zeros((1, D), np.float32)
        m["fmask"] = np.array([1.0 if r < j else 0.0 for r in range(8)], np.float32)
        maps.append(m)
    return maps


def kernel(**inputs):
    NT = NT_FULL
    nc, _ = build(NT)
    maps = make_in_maps(inputs, NT)
    res = run_bass_kernel_spmd(nc, maps, core_ids=list(range(8)))
    out = np.zeros((2, SEQ, D), np.float32)
    seg = NT * TOK
    for c in range(8):
        b, j = divmod(c, NSEG)
        out[b, j * seg:(j + 1) * seg] = res.results[c]["out"]
    return out
```
